# Optimizing a Trainium2 kernel written in Bass

```python
import math
import jax, jax.numpy as jnp
from jax import lax
import numpy as np

D_MODEL = 4096
BATCH = 2
SEQ = 4096
DEPTH = 2

N_MIXERS = 2
N_S5_LAYERS = (DEPTH + 1) // 2
N_ATT_LAYERS = DEPTH // 2
D_FF = 11008
RMS_EPS = 1e-6

S5_WIDTH = D_MODEL
S5_H = 16
S5_G = S5_WIDTH // S5_H
S5_P = 64
DT_MIN = 0.001
DT_MAX = 0.1

HEAD_DIM = 64
N_Q_HEADS = D_MODEL // HEAD_DIM
N_KV_HEADS = 8
Q_PER_KV = N_Q_HEADS // N_KV_HEADS
ATT_WIDTH = N_Q_HEADS * HEAD_DIM
KV_WIDTH = N_KV_HEADS * HEAD_DIM
QKV_DIM = ATT_WIDTH + 2 * KV_WIDTH
WINDOW = 128
BLOCK = 128
ROT_DIM = HEAD_DIM // 4
ROPE_THETA = 500000.0
NEG_INF = -1e30

kernel_name = "hybrid_s5_swa_sink_macaron_sandwich"


def rmsnorm(x, g):
    xf = x.astype(jnp.float32)
    xf = xf * lax.rsqrt(jnp.mean(xf * xf, axis=-1, keepdims=True) + RMS_EPS)
    return (xf * g.astype(jnp.float32)).astype(x.dtype)


def swiglu(x, w_gate, w_up, w_down):
    return (jax.nn.silu(x @ w_gate) * (x @ w_up)) @ w_down


def _complex_scan_op(e1, e2):
    a1r, a1i, b1r, b1i = e1
    a2r, a2i, b2r, b2i = e2
    ar = a2r * a1r - a2i * a1i
    ai = a2r * a1i + a2i * a1r
    br = a2r * b1r - a2i * b1i + b2r
    bi = a2r * b1i + a2i * b1r + b2i
    return (ar, ai, br, bi)


def s5_mixer(h, w_in, log_dt, a_re, a_im, b_re, b_im, c_re, c_im, d_skip, w_glu, b_glu, w_out):
    bsz, seq, _ = h.shape
    u = (h @ w_in).astype(jnp.float32).reshape(bsz, seq, S5_G, S5_H)
    dt = jnp.exp(log_dt.astype(jnp.float32))[:, None]
    ar = a_re.astype(jnp.float32)
    ai = a_im.astype(jnp.float32)
    mag = jnp.exp(dt * ar)
    ang = dt * ai
    abar_re = mag * jnp.cos(ang)
    abar_im = mag * jnp.sin(ang)
    den = ar * ar + ai * ai
    nr = abar_re - 1.0
    ni = abar_im
    f_re = ((nr * ar + ni * ai) / den)[..., None]
    f_im = ((ni * ar - nr * ai) / den)[..., None]
    br = b_re.astype(jnp.float32)
    bi = b_im.astype(jnp.float32)
    bbar_re = f_re * br - f_im * bi
    bbar_im = f_re * bi + f_im * br
    ut = jnp.swapaxes(u, 0, 1)
    bu_re = jnp.einsum('lbgh,gph->lbgp', ut, bbar_re)
    bu_im = jnp.einsum('lbgh,gph->lbgp', ut, bbar_im)
    a_re_l = jnp.broadcast_to(abar_re[None, None], (seq, 1, S5_G, S5_P))
    a_im_l = jnp.broadcast_to(abar_im[None, None], (seq, 1, S5_G, S5_P))
    _, _, s_re, s_im = lax.associative_scan(_complex_scan_op, (a_re_l, a_im_l, bu_re, bu_im), axis=0)
    y = (jnp.einsum('lbgp,ghp->blgh', s_re, c_re.astype(jnp.float32))
         - jnp.einsum('lbgp,ghp->blgh', s_im, c_im.astype(jnp.float32))
         + d_skip.astype(jnp.float32) * u)
    y = y.reshape(bsz, seq, S5_WIDTH).astype(h.dtype)
    z = jax.nn.gelu(y)
    z = z * jax.nn.sigmoid(z @ w_glu + b_glu)
    return z @ w_out


def rope_partial(t, cos, sin):
    r = ROT_DIM // 2
    t1 = t[..., :r]
    t2 = t[..., r:ROT_DIM]
    return jnp.concatenate([t1 * cos - t2 * sin, t2 * cos + t1 * sin, t[..., ROT_DIM:]], axis=-1)


def swa_sink_attention(h, w_qkv, b_qkv, sinks, w_o, b_o):
    bsz, seq, _ = h.shape
    nb = seq // BLOCK
    qkv = h @ w_qkv + b_qkv
    q = qkv[..., :ATT_WIDTH].reshape(bsz, seq, N_Q_HEADS, HEAD_DIM)
    k = qkv[..., ATT_WIDTH:ATT_WIDTH + KV_WIDTH].reshape(bsz, seq, N_KV_HEADS, HEAD_DIM)
    v = qkv[..., ATT_WIDTH + KV_WIDTH:].reshape(bsz, seq, N_KV_HEADS, HEAD_DIM)
    pos = jnp.arange(seq, dtype=jnp.float32)
    inv_freq = ROPE_THETA ** (-(jnp.arange(0, ROT_DIM, 2, dtype=jnp.float32) / ROT_DIM))
    ang = pos[:, None] * inv_freq[None, :]
    cos = jnp.cos(ang)[:, None, :].astype(h.dtype)
    sin = jnp.sin(ang)[:, None, :].astype(h.dtype)
    q = rope_partial(q, cos, sin)
    k = rope_partial(k, cos, sin)
    qb = q.reshape(bsz, nb, BLOCK, N_KV_HEADS, Q_PER_KV, HEAD_DIM)
    kb = k.reshape(bsz, nb, BLOCK, N_KV_HEADS, HEAD_DIM)
    vb = v.reshape(bsz, nb, BLOCK, N_KV_HEADS, HEAD_DIM)
    pad = ((0, 0), (1, 0), (0, 0), (0, 0), (0, 0))
    k2 = jnp.concatenate([jnp.pad(kb, pad)[:, :-1], kb], axis=2)
    v2 = jnp.concatenate([jnp.pad(vb, pad)[:, :-1], vb], axis=2)
    scale = 1.0 / math.sqrt(HEAD_DIM)
    s = jnp.einsum('bnqhgd,bnkhd->bnhgqk', qb.astype(jnp.float32) * scale, k2.astype(jnp.float32))
    blk = jnp.arange(nb)[:, None, None]
    qpos = blk * BLOCK + jnp.arange(BLOCK)[None, :, None]
    kpos = (blk - 1) * BLOCK + jnp.arange(2 * BLOCK)[None, None, :]
    diff = qpos - kpos
    valid = (diff >= 0) & (diff < WINDOW) & (kpos >= 0)
    s = jnp.where(valid[None, :, None, None], s, NEG_INF)
    sink = sinks.astype(jnp.float32).reshape(1, 1, N_KV_HEADS, Q_PER_KV, 1, 1)
    m = jnp.maximum(jnp.max(s, axis=-1, keepdims=True), sink)
    p = jnp.exp(s - m)
    p = p / (jnp.sum(p, axis=-1, keepdims=True) + jnp.exp(sink - m))
    o = jnp.einsum('bnhgqk,bnkhd->bnqhgd', p.astype(v2.dtype), v2)
    o = o.reshape(bsz, seq, ATT_WIDTH)
    return o @ w_o + b_o


def setup_inputs(seed: int = 0) -> dict:
    key = jax.random.key(seed)
    ks = jax.random.split(key, 24)
    f32 = jnp.float32
    x = jax.random.normal(ks[0], (BATCH, SEQ, D_MODEL), f32)
    norm_g = 1.0 + 0.02 * jax.random.normal(ks[1], (DEPTH, 6, D_MODEL), f32)
    ffn_w_gate = jax.random.normal(ks[2], (DEPTH, 2, D_MODEL, D_FF), f32) * D_MODEL ** -0.5
    ffn_w_up = jax.random.normal(ks[3], (DEPTH, 2, D_MODEL, D_FF), f32) * D_MODEL ** -0.5
    ffn_w_down = jax.random.normal(ks[4], (DEPTH, 2, D_FF, D_MODEL), f32) * D_FF ** -0.5
    ns = N_S5_LAYERS
    s5_w_in = jax.random.normal(ks[5], (ns, D_MODEL, S5_WIDTH), f32) * D_MODEL ** -0.5
    s5_log_dt = jax.random.uniform(ks[6], (ns, S5_G), f32, math.log(DT_MIN), math.log(DT_MAX))
    n_idx = jnp.arange(S5_P, dtype=f32)
    s5_a_re = -0.5 + 0.01 * jax.random.normal(ks[7], (ns, S5_G, S5_P), f32)
    s5_a_im = math.pi * n_idx + 0.01 * jax.random.normal(ks[8], (ns, S5_G, S5_P), f32)
    s5_b_re = jax.random.normal(ks[9], (ns, S5_G, S5_P, S5_H), f32) * (2 * S5_H) ** -0.5
    s5_b_im = jax.random.normal(ks[10], (ns, S5_G, S5_P, S5_H), f32) * (2 * S5_H) ** -0.5
    s5_c_re = jax.random.normal(ks[11], (ns, S5_G, S5_H, S5_P), f32) * (2 * S5_P) ** -0.5
    s5_c_im = jax.random.normal(ks[12], (ns, S5_G, S5_H, S5_P), f32) * (2 * S5_P) ** -0.5
    s5_d = jax.random.normal(ks[13], (ns, S5_G, S5_H), f32)
    s5_w_glu = jax.random.normal(ks[14], (ns, S5_WIDTH, S5_WIDTH), f32) * S5_WIDTH ** -0.5
    s5_b_glu = 0.02 * jax.random.normal(ks[15], (ns, S5_WIDTH), f32)
    s5_w_out = jax.random.normal(ks[16], (ns, S5_WIDTH, D_MODEL), f32) * S5_WIDTH ** -0.5
    na = N_ATT_LAYERS
    attn_w_qkv = jax.random.normal(ks[17], (na, D_MODEL, QKV_DIM), f32) * D_MODEL ** -0.5
    attn_b_qkv = 0.02 * jax.random.normal(ks[18], (na, QKV_DIM), f32)
    attn_sinks = 0.5 * jax.random.normal(ks[19], (na, N_Q_HEADS), f32)
    attn_w_o = jax.random.normal(ks[20], (na, ATT_WIDTH, D_MODEL), f32) * ATT_WIDTH ** -0.5
    attn_b_o = 0.02 * jax.random.normal(ks[21], (na, D_MODEL), f32)
    return {"x": x, "norm_g": norm_g, "ffn_w_gate": ffn_w_gate, "ffn_w_up": ffn_w_up,
            "ffn_w_down": ffn_w_down, "s5_w_in": s5_w_in, "s5_log_dt": s5_log_dt,
            "s5_a_re": s5_a_re, "s5_a_im": s5_a_im, "s5_b_re": s5_b_re, "s5_b_im": s5_b_im,
            "s5_c_re": s5_c_re, "s5_c_im": s5_c_im, "s5_d": s5_d, "s5_w_glu": s5_w_glu,
            "s5_b_glu": s5_b_glu, "s5_w_out": s5_w_out, "attn_w_qkv": attn_w_qkv,
            "attn_b_qkv": attn_b_qkv, "attn_sinks": attn_sinks, "attn_w_o": attn_w_o,
            "attn_b_o": attn_b_o}


def reference(x, norm_g, ffn_w_gate, ffn_w_up, ffn_w_down, s5_w_in, s5_log_dt, s5_a_re, s5_a_im,
              s5_b_re, s5_b_im, s5_c_re, s5_c_im, s5_d, s5_w_glu, s5_b_glu, s5_w_out,
              attn_w_qkv, attn_b_qkv, attn_sinks, attn_w_o, attn_b_o):
    h = x
    for i in range(DEPTH):
        g = norm_g[i]
        f = swiglu(rmsnorm(h, g[0]), ffn_w_gate[i, 0], ffn_w_up[i, 0], ffn_w_down[i, 0])
        h = h + 0.5 * rmsnorm(f, g[1])
        hn = rmsnorm(h, g[2])
        j = i // N_MIXERS
        if i % N_MIXERS == 0:
            mix = s5_mixer(hn, s5_w_in[j], s5_log_dt[j], s5_a_re[j], s5_a_im[j], s5_b_re[j],
                           s5_b_im[j], s5_c_re[j], s5_c_im[j], s5_d[j], s5_w_glu[j],
                           s5_b_glu[j], s5_w_out[j])
        else:
            mix = swa_sink_attention(hn, attn_w_qkv[j], attn_b_qkv[j], attn_sinks[j],
                                     attn_w_o[j], attn_b_o[j])
        h = h + rmsnorm(mix, g[3])
        f = swiglu(rmsnorm(h, g[4]), ffn_w_gate[i, 1], ffn_w_up[i, 1], ffn_w_down[i, 1])
        h = h + 0.5 * rmsnorm(f, g[5])
    return h
```

```python
import math
import numpy as np
import ml_dtypes
import concourse.bass as bass
import concourse.mybir as mybir
from concourse.bass_utils import run_bass_kernel_spmd

F32 = mybir.dt.float32
BF16 = mybir.dt.bfloat16
I32 = mybir.dt.int32
AF = mybir.ActivationFunctionType
ALU = mybir.AluOpType
AX = mybir.AxisListType

NCORES = 8
TOK = 1024
TT = 512
RMS_EPS = 1e-6
TWO_PI = 2.0 * math.pi


class Tok:
    __slots__ = ("sem", "val", "eng")

    def __init__(self, sem, val, eng):
        self.sem, self.val, self.eng = sem, val, eng


class Ctx:
    def __init__(self, nc, stack):
        self.nc = nc
        self.h = {"pe": nc.tensor, "act": nc.scalar, "dve": nc.vector, "pool": nc.gpsimd,
                  "sp": nc.sync}
        self.sem = {}
        self.cnt = {}
        for e in ("pe", "act", "dve", "pool"):
            self.sem[e] = stack.enter_context(nc.semaphore("s_" + e))
            self.cnt[e] = 0
        self.dsem = {}
        self.dcnt = {}
        self.dnext = {}
        for q in ("sp", "pool"):
            self.dsem[q] = [stack.enter_context(nc.semaphore("d_%s%d" % (q, i))) for i in range(8)]
            self.dcnt[q] = [0] * 8
            self.dnext[q] = 0
        self.stack = stack
        self.last_w = {}
        self.readers = {}
        self.waited = {e: {} for e in self.h}
        self.all_toks = {}
        self.streams = {e: [] for e in self.h}

    def _wait(self, eng, tok):
        if tok is None:
            return
        w = self.waited[eng]
        key = id(tok.sem)
        if w.get(key, 0) >= tok.val:
            return
        self.streams[eng].append(lambda h, s_=tok.sem, v=tok.val: h.wait_ge(s_, v))
        w[key] = tok.val

    def _deps(self, eng, reads, writes):
        toks = []
        for r in reads:
            t = self.last_w.get(r)
            if t is not None:
                toks.append(t)
        for w_ in writes:
            t = self.last_w.get(w_)
            if t is not None:
                toks.append(t)
            toks.extend(self.readers.get(w_, ()))
        for t in toks:
            if t.eng == "pe" and eng == "pe":
                continue
            self._wait(eng, t)

    def _commit(self, tok, reads, writes):
        for r in reads:
            self.readers.setdefault(r, []).append(tok)
        for w_ in writes:
            self.last_w[w_] = tok
            self.readers[w_] = []
        self.all_toks[id(tok.sem)] = tok

    def op(self, eng, fn, reads=(), writes=()):
        self._deps(eng, reads, writes)
        self.cnt[eng] += 1
        tok = Tok(self.sem[eng], self.cnt[eng], eng)
        self.streams[eng].append(lambda h, fn=fn, s_=tok.sem: fn(h).then_inc(s_, 1))
        self._commit(tok, reads, writes)
        return tok

    def mm(self, fns, reads=(), writes=()):
        self._deps("pe", reads, writes)
        self.cnt["pe"] += 1
        tok = Tok(self.sem["pe"], self.cnt["pe"], "pe")
        fns = list(fns)
        for fn in fns[:-1]:
            self.streams["pe"].append(fn)
        self.streams["pe"].append(lambda h, fn=fns[-1], s_=tok.sem: fn(h).then_inc(s_, 1))
        self._commit(tok, reads, writes)
        return tok

    def dma(self, q, out, in_, reads=(), writes=(), slow=False):
        self._deps(q, reads, writes)
        i = self.dnext[q]
        self.dnext[q] = (i + 1) % 8
        sem = self.dsem[q][i]
        if self.dcnt[q][i] > 0:
            self._wait(q, Tok(sem, self.dcnt[q][i], "dma"))
        self.dcnt[q][i] += 16
        tok = Tok(sem, self.dcnt[q][i], "dma")
        if slow:
            self.streams[q].append(lambda h: h.dma_start(
                out=out, in_=in_, allow_slow_non_contiguous=True).then_inc(sem, 16))
        else:
            self.streams[q].append(lambda h: h.dma_start(out=out, in_=in_).then_inc(sem, 16))
        self._commit(tok, reads, writes)
        return tok

    def collective(self, src, dst, reads=(), writes=()):
        self._deps("pool", reads, writes)
        sem = self.stack.enter_context(self.nc.semaphore())
        self.streams["pool"].append(lambda h: h.collective_compute(
            "AllGather", ALU.bypass, [list(range(NCORES))],
            ins=[src.ap().opt()], outs=[dst.ap().opt()]).then_inc(sem))
        tok = Tok(sem, 1, "cc")
        self._commit(tok, reads, writes)
        return tok

    def barrier(self):
        toks = list(self.all_toks.values())
        for e in self.h:
            for t in toks:
                self._wait(e, t)
        self.last_w = {}
        self.readers = {}

    def finish(self, tok):
        self._wait("sp", tok)


DEBUG = False


class Cfg:
    def __init__(self, D=4096, FF=11008):
        self.D, self.FF = D, FF
        self.KT = D // 128
        self.KF = FF // 128
        self.G = D // 16
        self.NH = D // 64
        self.NKV = 8
        self.QPK = self.NH // self.NKV
        self.KVW = self.NKV * 64
        self.QKV = D + 2 * self.KVW


BIG = ["wg00", "wu00", "wd00", "s5in", "s5glu", "s5out", "wg01", "wu01", "wd01",
       "wg10", "wu10", "wd10", "wqkv", "wo", "wg11", "wu11", "wd11"]


def big_shapes(cfg):
    D, FF = cfg.D, cfg.FF
    sh = {}
    for i in range(2):
        for s_ in range(2):
            sh["wg%d%d" % (i, s_)] = (D, FF)
            sh["wu%d%d" % (i, s_)] = (D, FF)
            sh["wd%d%d" % (i, s_)] = (FF, D)
    sh["s5in"] = (D, D)
    sh["s5glu"] = (D, D)
    sh["s5out"] = (D, D)
    sh["wqkv"] = (D, cfg.QKV)
    sh["wo"] = (D, D)
    return sh


class _UniqNc:
    def __init__(self, nc):
        object.__setattr__(self, "_nc", nc)
        object.__setattr__(self, "_n", 0)

    def __getattr__(self, k):
        return getattr(self._nc, k)

    def sbuf_tensor(self, name, shape, dtype):
        object.__setattr__(self, "_n", self._n + 1)
        return self._nc.sbuf_tensor("%s_%d" % (name, self._n), shape, dtype)


class Prog:
    def __init__(self, cfg, stages, big_needed, use_cc=True):
        from contextlib import ExitStack
        self.cfg = cfg
        self.stack = ExitStack()
        nc = self.nc = _UniqNc(bass.Bass("TRN2", target_bir_lowering=False))
        D, FF = cfg.D, cfg.FF
        self.x = nc.dram_tensor("x", [TOK, D], F32, kind="ExternalInput").ap()
        self.out = nc.dram_tensor("out", [TOK, D], F32, kind="ExternalOutput").ap()
        self.normg = nc.dram_tensor("norm_g", [12, D], F32, kind="ExternalInput").ap()
        self.ident_in = nc.dram_tensor("ident", [128, 128], F32, kind="ExternalInput").ap()
        sh = big_shapes(cfg)
        self.wsh, self.wfull = {}, {}
        self.use_cc = use_cc
        if use_cc:
            for n in big_needed:
                K, N = sh[n]
                self.wsh[n] = nc.dram_tensor(n, [K // NCORES, N], F32, kind="ExternalInput")
            for n in big_needed:
                K, N = sh[n]
                self.wfull[n] = nc.dram_tensor("F_" + n, [K, N], F32)
            self.wb = {n: nc.dram_tensor("B_" + n, [sh[n][0] // NCORES, sh[n][1]], F32) for n in big_needed}
        else:
            for n in big_needed:
                K, N = sh[n]
                self.wfull[n] = nc.dram_tensor(n, [K, N], F32, kind="ExternalInput")
        self.h = nc.dram_tensor("h_scr", [TOK, D], F32)
        self.cst = {}
        for n, shp in (("iota", [128, TOK]), ("iota1", [128, TOK]), ("msk", [128, 16]), ("sel", [128, NCORES * 3])):
            self.cst[n] = nc.dram_tensor(n, shp, F32, kind="ExternalInput").ap()
        G = cfg.G
        self.s5in = {}
        for n, sz in (("log_dt", G), ("a_re", G * 64), ("a_im", G * 64), ("b_re", G * 1024), ("b_im", G * 1024),
                      ("c_re", G * 1024), ("c_im", G * 1024), ("d", D), ("bglu", D)):
            self.s5in[n] = nc.dram_tensor("s5_" + n, [sz], F32, kind="ExternalInput").ap()
        for n, shp in (("hsel", [128, NCORES]), ("amask", [128, 512]), ("cosT", [128, TOK]), ("sinT", [128, TOK]),
                       ("rotT", [128, 128])):
            self.cst[n] = nc.dram_tensor(n, shp, F32, kind="ExternalInput").ap()
        self.attin = {}
        for n, sz in (("bqkv", cfg.QKV), ("sinks", cfg.NH), ("bo", D)):
            self.attin[n] = nc.dram_tensor("at_" + n, [sz], F32, kind="ExternalInput").ap()
        HW_ = cfg.NKV * 128 + cfg.KVW
        self.halo_b = nc.dram_tensor("halo_b", [128, HW_], F32)
        self.halo_all = nc.dram_tensor("halo_all", [NCORES * 128, HW_], F32)
        self.yscr = nc.dram_tensor("yscr", [D, TOK], F32)
        self.wend_b = nc.dram_tensor("wend_b", [128, G], F32)
        self.wend_all = nc.dram_tensor("wend_all", [NCORES * 128, G], F32)
        self.dbg_t = nc.dram_tensor("dbg", [128, 4096], F32, kind="ExternalOutput").ap() if DEBUG else None
        self.dbg_off = 0
        self.big_needed = big_needed
        self.stages = stages

    def build(self):
        nc, cfg = self.nc, self.cfg
        st = self.stack
        with st:
            blk = st.enter_context(nc.Block())
            self.ctx = ctx = Ctx(nc, st)
            self.ps = [st.enter_context(nc.psum_tensor("ps%d" % b, [128, 512], F32)) for b in range(8)]
            self.ident = st.enter_context(nc.sbuf_tensor("identb", [128, 128], BF16))
            self.identf = st.enter_context(nc.sbuf_tensor("identf", [128, 128], F32))
            self.wtok = {}
            if DEBUG:
                self.dbg_stage = st.enter_context(nc.sbuf_tensor("dbg_stage", [128, 512], F32))

            self.emit()
            S = ctx.streams

            @blk.gpsimd
            def _(h):
                for f in S["pool"]:
                    f(h)

            @blk.sync
            def _(h):
                for f in S["sp"]:
                    f(h)

            @blk.tensor
            def _(h):
                for f in S["pe"]:
                    f(h)

            @blk.vector
            def _(h):
                for f in S["dve"]:
                    f(h)

            @blk.scalar
            def _(h):
                for f in S["act"]:
                    f(h)
            print("instr counts", {k: len(v) for k, v in S.items()}, flush=True)
        return nc._nc

    def emit(self):
        ctx, nc, cfg = self.ctx, self.nc, self.cfg
        if self.use_cc:
            for n in self.big_needed:
                ctx.dma("sp", self.wb[n].ap(), self.wsh[n].ap(), writes=[("wb", n)])
            for n in self.big_needed:
                self.wtok[n] = ctx.collective(self.wb[n], self.wfull[n], reads=[("wb", n)],
                                              writes=[("wfull", n)])
        ctx.dma("sp", self.identf[:, :], self.ident_in, writes=["identf"])
        ctx.op("dve", lambda e: e.tensor_copy(out=self.ident[:, :], in_=self.identf[:, :]),
               reads=["identf"], writes=["ident"])
        h_src = self.x
        last = None
        for stg in self.stages:
            kind = stg[0]
            if kind == "ffn":
                _, i, s_ = stg
                dst = self.h.ap()
                last = self.ffn(h_src, dst, 6 * i + (0 if s_ == 0 else 4),
                                "wg%d%d" % (i, s_), "wu%d%d" % (i, s_), "wd%d%d" % (i, s_))
                h_src = dst
            elif kind == "attn":
                dst = self.h.ap()
                self.attn(h_src, dst, 8)
                h_src = dst
            elif kind == "s5":
                dst = self.h.ap()
                self.s5(h_src, dst, 2)
                h_src = dst
            ctx.barrier()
        t = ctx.dma("sp", self.out, h_src, reads=["hdram"], writes=["out"])
        ctx.finish(t)

    def prenorm(self, h_src, tt, gidx, R, names):
        ctx, cfg = self.ctx, self.cfg
        D, KT = cfg.D, cfg.KT
        xnT, hb, xb, gcol, small = R["xnT"], R["hb"], R["xb"], R["gcol"], R["small"]
        ctx.dma("sp", gcol[:, 0:KT], self.normg[gidx, :].rearrange("(k p) -> p k", p=128),
                writes=["gcol"], slow=True)
        for ts in range(4):
            r0 = tt * TT + ts * 128
            ctx.dma("sp", hb[:, :], h_src[r0:r0 + 128, :], reads=["hdram"], writes=["hb"])
            ctx.op("act", lambda e: e.activation(out=xb[:, :], in_=hb[:, :], func=AF.Square,
                                                 accum_out=small[:, 0:1]),
                   reads=["hb"], writes=["xb", "ss"])
            ctx.op("dve", lambda e: e.tensor_scalar(out=small[:, 1:2], in0=small[:, 0:1],
                                                    scalar1=1.0 / D, scalar2=RMS_EPS,
                                                    op0=ALU.mult, op1=ALU.add),
                   reads=["ss"], writes=["t1"])
            ctx.op("act", lambda e: e.activation(out=small[:, 2:3], in_=small[:, 1:2], func=AF.Sqrt),
                   reads=["t1"], writes=["t2"])
            ctx.op("dve", lambda e: e.reciprocal(out=small[:, 3:4], in_=small[:, 2:3]),
                   reads=["t2"], writes=["rstd"])
            ctx.op("dve", lambda e: e.tensor_scalar(out=xb[:, :], in0=hb[:, :],
                                                    scalar1=small[:, 3:4], scalar2=None,
                                                    op0=ALU.mult),
                   reads=["hb", "rstd"], writes=["xb"])
            for k0 in range(0, KT, 8):
                kn = min(8, KT - k0)
                bank = 6 + ((k0 // 8) % 2)
                pT = self.ps[bank][:, :].bitcast(BF16)
                ctx.mm([(lambda e, kk=kk, pT=pT, k0=k0: e.transpose(out=pT[:, kk * 128:(kk + 1) * 128],
                                                      in_=xb[:, (k0 + kk) * 128:(k0 + kk + 1) * 128],
                                                      identity=self.ident[:, :]))
                        for kk in range(kn)],
                       reads=["xb", "ident"], writes=[("ps", bank)])
                for kk in range(kn):
                    k = k0 + kk
                    ctx.op("dve", lambda e, kk=kk, k=k, ts=ts, pT=pT: e.tensor_scalar(
                        out=xnT[:, k, ts * 128:(ts + 1) * 128], in0=pT[:, kk * 128:(kk + 1) * 128],
                        scalar1=gcol[:, k:k + 1], scalar2=None, op0=ALU.mult),
                        reads=[("ps", bank), "gcol"], writes=[("xnT", k)])

    def dbg(self, ap, n, nm=None):
        if not DEBUG or (DEBUG is not True and nm not in DEBUG):
            self.dbg_off += n
            return
        self.ctx.barrier()
        print("dbg", self.dbg_off, n)
        self.ctx.op("dve", lambda e: e.tensor_copy(out=self.dbg_stage[:, 0:n], in_=ap), writes=["dbgs"])
        self.ctx.dma("sp", self.dbg_t[:, self.dbg_off:self.dbg_off + n], self.dbg_stage[:, 0:n],
                     reads=["dbgs"], writes=["dbg"])
        self.dbg_off += n
        self.ctx.barrier()

    def wtile(self, name, r0k, kn, c0, cw, wbuf, slot):
        W = self.wfull[name].ap().rearrange("(k p) c -> p k c", p=128)
        return self.ctx.dma("pool", wbuf[:, slot, 0:kn, 0:cw], W[:, r0k:r0k + kn, c0:c0 + cw],
                            reads=[("wfull", name)], writes=[("wbuf", slot)])

    def ffn(self, h_src, h_dst, gidx, wg, wu, wd):
        ctx, cfg, nc = self.ctx, self.cfg, self.nc
        D, FF, KT, KF = cfg.D, cfg.FF, cfg.KT, cfg.KF
        from contextlib import ExitStack
        NW = 3
        with ExitStack() as st:
            act_sb = st.enter_context(nc.sbuf_tensor("act_sb", [128, KF, TT], BF16))
            R64 = st.enter_context(nc.sbuf_tensor("R64", [128, 4 * D], F32))
            wbuf = st.enter_context(nc.sbuf_tensor("wbuf", [128, NW, 8, 512], BF16))
            sg = st.enter_context(nc.sbuf_tensor("sg", [128, 4, TT], F32))
            hc = st.enter_context(nc.sbuf_tensor("hc", [128, 2, 1024], F32))
            gc = st.enter_context(nc.sbuf_tensor("gc", [128, 1024], F32))
            small = st.enter_context(nc.sbuf_tensor("small", [128, 64], F32))
            gcol = st.enter_context(nc.sbuf_tensor("gcol", [128, 32], F32))
            Rb = R64[:, :].bitcast(BF16)
            xnT = Rb[:, 0:KT * TT].rearrange("p (k t) -> p k t", t=TT)
            hb = R64[:, 2 * D:3 * D]
            xb = Rb[:, 6 * D:7 * D]
            f_sb = R64[:, :].rearrange("p (s d) -> p s d", d=D)
            R = {"xnT": xnT, "hb": hb, "xb": xb, "gcol": gcol, "small": small}
            slot = [0]
            ctx.op("dve", lambda e: e.memset(small[:, :], 0.0), writes=["ss", "t1", "t2", "rstd", "tot"])

            def next_slot():
                s_ = slot[0]
                slot[0] = (s_ + 1) % NW
                return s_

            cbs = [(c0, min(512, FF - c0)) for c0 in range(0, FF, 512)]
            last = None
            for tt in range(TOK // TT):
                self.prenorm(h_src, tt, gidx, R, None)
                if tt == 0:
                    self.dbg(hb, 512, "hb")
                    self.dbg(small[:, 0:8], 8, "small")
                    self.dbg(xb, 512, "xb")
                    self.dbg(xnT[:, 0, :], 512, "xnT")
                for (c0, cw) in cbs:
                    nj = cw // 128
                    for which, wname, b0 in (("g", wg, 0), ("u", wu, 4)):
                        for ks in range(0, KT, 8):
                            kn = min(8, KT - ks)
                            sl = next_slot()
                            self.wtile(wname, ks, kn, c0, cw, wbuf, sl)
                            for j in range(nj):
                                ctx.mm([(lambda e, kk=kk, j=j, sl=sl, ks=ks, b0=b0: e.matmul(
                                    self.ps[b0 + j][:, :], lhsT=wbuf[:, sl, kk, j * 128:(j + 1) * 128],
                                    rhs=xnT[:, ks + kk, :], start=(ks + kk == 0),
                                    stop=(ks + kk == KT - 1))) for kk in range(kn)],
                                    reads=[("wbuf", sl)] + [("xnT", ks + kk) for kk in range(kn)],
                                    writes=[("ps", b0 + j)])
                        for j in range(nj):
                            if which == "g":
                                ctx.op("act", lambda e, j=j: e.activation(
                                    out=sg[:, j, :], in_=self.ps[j][:, :], func=AF.Silu),
                                    reads=[("ps", j)], writes=[("sg", j)])
                            else:
                                jt = c0 // 128 + j
                                ctx.op("dve", lambda e, j=j, jt=jt: e.tensor_tensor(
                                    out=act_sb[:, jt, :], in0=sg[:, j, :], in1=self.ps[4 + j][:, :],
                                    op=ALU.mult),
                                    reads=[("sg", j), ("ps", 4 + j)], writes=[("act", jt)])
                if tt == 0:
                    self.dbg(act_sb[:, 0, :], 512, "act")
                for n in range(D // 512):
                    bb = (n % 2) * 4
                    for js in range(0, KF, 8):
                        jn = min(8, KF - js)
                        sl = next_slot()
                        self.wtile(wd, js, jn, n * 512, 512, wbuf, sl)
                        for ts in range(4):
                            ctx.mm([(lambda e, jj=jj, ts=ts, sl=sl, js=js, bb=bb: e.matmul(
                                self.ps[bb + ts][:, :], lhsT=act_sb[:, js + jj, ts * 128:(ts + 1) * 128],
                                rhs=wbuf[:, sl, jj, :], start=(js + jj == 0),
                                stop=(js + jj == KF - 1))) for jj in range(jn)],
                                reads=[("wbuf", sl)] + [("act", js + jj) for jj in range(jn)],
                                writes=[("ps", bb + ts)])
                    for ts in range(4):
                        ctx.op("dve", lambda e, ts=ts, n=n, bb=bb: e.tensor_copy(
                            out=f_sb[:, ts, n * 512:(n + 1) * 512], in_=self.ps[bb + ts][:, :]),
                            reads=[("ps", bb + ts)], writes=[("f", ts, n)])
                        ctx.op("act", lambda e, ts=ts, n=n: e.activation(
                            out=sg[:, 0, :], in_=f_sb[:, ts, n * 512:(n + 1) * 512], func=AF.Square,
                            accum_out=small[:, 8 + ts * 8 + n:9 + ts * 8 + n]),
                            reads=[("f", ts, n)], writes=[("sg", 0), ("ssq", ts, n)])
                if tt == 0:
                    self.dbg(f_sb[:, 0, 0:512], 512, "f")
                    self.dbg(small[:, 0:64], 64, "small2")
                last = self.postnorm_residual(h_src, h_dst, tt, gidx + 1, 0.5, f_sb, small, hc, gc)
                ctx.barrier()
        return last

    def postnorm_residual(self, h_src, h_dst, tt, gidx, coef, f_sb, small, hc, gc):
        ctx, cfg = self.ctx, self.cfg
        D = cfg.D
        NB = D // 512
        CH = min(1024, D)
        last = None
        for ts in range(4):
            r0 = tt * TT + ts * 128
            ctx.op("dve", lambda e, ts=ts: e.tensor_reduce(
                out=small[:, 4:5], in_=small[:, 8 + ts * 8:8 + ts * 8 + NB], op=ALU.add, axis=AX.X),
                reads=[("ssq", ts, n) for n in range(NB)], writes=["tot"])
            ctx.op("dve", lambda e: e.tensor_scalar(out=small[:, 5:6], in0=small[:, 4:5],
                                                    scalar1=1.0 / D, scalar2=RMS_EPS,
                                                    op0=ALU.mult, op1=ALU.add),
                   reads=["tot"], writes=["t1"])
            ctx.op("act", lambda e: e.activation(out=small[:, 6:7], in_=small[:, 5:6], func=AF.Sqrt),
                   reads=["t1"], writes=["t2"])
            ctx.op("dve", lambda e: e.reciprocal(out=small[:, 7:8], in_=small[:, 6:7]),
                   reads=["t2"], writes=["rstd"])
            for c in range(D // CH):
                cs = slice(c * CH, (c + 1) * CH)
                hs = c % 2
                ctx.dma("sp", hc[:, hs, 0:CH], h_src[r0:r0 + 128, cs], reads=["hdram"],
                        writes=[("hc", hs)])
                ctx.dma("sp", gc[:, 0:CH], self.normg[gidx:gidx + 1, cs].broadcast_to([128, CH]),
                        writes=["gc"])
                ctx.op("dve", lambda e, ts=ts, cs=cs: e.scalar_tensor_tensor(
                    out=f_sb[:, ts, cs], in0=f_sb[:, ts, cs], scalar=small[:, 7:8], in1=gc[:, 0:CH],
                    op0=ALU.mult, op1=ALU.mult),
                    reads=[("f", ts, n) for n in range(NB)] + ["rstd", "gc"], writes=[("f2", ts, c)])
                ctx.op("dve", lambda e, ts=ts, cs=cs, hs=hs: e.scalar_tensor_tensor(
                    out=f_sb[:, ts, cs], in0=f_sb[:, ts, cs], scalar=coef, in1=hc[:, hs, 0:CH],
                    op0=ALU.mult, op1=ALU.add),
                    reads=[("f2", ts, c), ("hc", hs)], writes=[("f3", ts, c)])
            last = ctx.dma("sp", h_dst[r0:r0 + 128, :], f_sb[:, ts, :],
                           reads=[("f3", ts, c) for c in range(D // CH)], writes=["hdram_w"])
        return last


    def linear_A(self, wname, K_tiles, col0, ncols, rhs_fn, rhs_res, evac, wbuf, next_slot, ntok=TT, bw=512):
        ctx = self.ctx
        blk = 0
        for c0 in range(col0, col0 + ncols, bw):
            cw = min(bw, col0 + ncols - c0)
            nj = cw // 128
            b0 = (blk % 2) * (bw // 128)
            blk += 1
            for ks in range(0, K_tiles, 8):
                kn = min(8, K_tiles - ks)
                sl = next_slot()
                self.wtile(wname, ks, kn, c0, cw, wbuf, sl)
                for j in range(nj):
                    ctx.mm([(lambda e, kk=kk, j=j, sl=sl, ks=ks, b0=b0: e.matmul(
                        self.ps[b0 + j][:, 0:ntok], lhsT=wbuf[:, sl, kk, j * 128:(j + 1) * 128],
                        rhs=rhs_fn(ks + kk), start=(ks + kk == 0), stop=(ks + kk == K_tiles - 1)))
                        for kk in range(kn)],
                        reads=[("wbuf", sl)] + [(rhs_res, ks + kk) for kk in range(kn)],
                        writes=[("ps", b0 + j)])
            for j in range(nj):
                evac((c0 - col0) // 128 + j, b0 + j)

    def linear_B(self, wname, K_tiles, col0, ncols, lhs_fn, lhs_res, evac, wbuf, next_slot, nts=4):
        ctx = self.ctx
        nb = 0
        for c0 in range(col0, col0 + ncols, 512):
            cw = min(512, col0 + ncols - c0)
            bb = (nb % 2) * 4
            for js in range(0, K_tiles, 8):
                jn = min(8, K_tiles - js)
                sl = next_slot()
                self.wtile(wname, js, jn, c0, cw, wbuf, sl)
                for ts in range(nts):
                    ctx.mm([(lambda e, jj=jj, ts=ts, sl=sl, js=js, bb=bb, cw=cw: e.matmul(
                        self.ps[bb + ts][:, 0:cw], lhsT=lhs_fn(js + jj, ts),
                        rhs=wbuf[:, sl, jj, 0:cw], start=(js + jj == 0),
                        stop=(js + jj == K_tiles - 1))) for jj in range(jn)],
                        reads=[("wbuf", sl)] + [(lhs_res, js + jj) for jj in range(jn)],
                        writes=[("ps", bb + ts)])
            for ts in range(nts):
                evac(nb, ts, bb + ts, cw)
            nb += 1

    def sincos(self, turns_fn, n, sin_out, cos_out, ki, fr, halfpi, tag):
        ctx = self.ctx
        for (dst, add) in ((sin_out, 0.0), (cos_out, 0.25)):
            ctx.op("dve", lambda e, add=add: turns_fn(e, ki[:, 0:n], add), reads=[tag + "_in"],
                   writes=["kibuf"])
            ctx.op("dve", lambda e, add=add: turns_fn(e, fr[:, 0:n], add), reads=[tag + "_in"],
                   writes=["frbuf"])
            ctx.op("dve", lambda e: e.tensor_tensor(out=fr[:, 0:n], in0=fr[:, 0:n], in1=ki[:, 0:n],
                                                    op=ALU.subtract),
                   reads=["kibuf", "frbuf"], writes=["frbuf"])
            ctx.op("act", lambda e, dst=dst: e.activation(out=dst, in_=fr[:, 0:n], func=AF.Sin,
                                                          scale=TWO_PI),
                   reads=["frbuf"], writes=[tag + "_out"])

    def s5(self, h_src, h_dst, gidx):
        ctx, cfg, nc = self.ctx, self.cfg, self.nc
        D, KT, G = cfg.D, cfg.KT, cfg.G
        GP = G // 2
        from contextlib import ExitStack
        NW = 3
        S = self.s5in
        with ExitStack() as st0:
            uz = st0.enter_context(nc.sbuf_tensor("uz", [128, KT, TOK], BF16))
            small = st0.enter_context(nc.sbuf_tensor("small", [128, 64], F32))
            gcol = st0.enter_context(nc.sbuf_tensor("gcol", [128, 32], F32))
            dcol = st0.enter_context(nc.sbuf_tensor("dcol", [128, 32], F32))
            bglu = st0.enter_context(nc.sbuf_tensor("bglu", [128, 32], F32))
            slot = [0]

            def next_slot():
                s_ = slot[0]
                slot[0] = (s_ + 1) % NW
                return s_
            ctx.op("dve", lambda e: e.memset(small[:, :], 0.0), writes=["ss", "t1", "t2", "rstd", "tot"])
            ctx.dma("sp", dcol[:, 0:KT], S["d"].rearrange("(k p) -> p k", p=128), writes=["dcol"], slow=True)
            ctx.dma("sp", bglu[:, 0:KT], S["bglu"].rearrange("(k p) -> p k", p=128), writes=["bglu"], slow=True)
            with ExitStack() as st:
                R64 = st.enter_context(nc.sbuf_tensor("R64", [128, 7 * D // 2], F32))
                wbuf = st.enter_context(nc.sbuf_tensor("wbuf", [128, NW, 8, 512], BF16))
                Rb = R64[:, :].bitcast(BF16)
                xnT = Rb[:, 0:KT * TT].rearrange("p (k t) -> p k t", t=TT)
                hb = R64[:, 2 * D:3 * D]
                xb = Rb[:, 6 * D:7 * D]
                R = {"xnT": xnT, "hb": hb, "xb": xb, "gcol": gcol, "small": small}
                for tt in range(TOK // TT):
                    self.prenorm(h_src, tt, gidx, R, None)

                    def evac(jt, bank, tt=tt):
                        ctx.op("act", lambda e: e.activation(
                            out=uz[:, jt, tt * TT:(tt + 1) * TT], in_=self.ps[bank][:, :], func=AF.Copy),
                            reads=[("ps", bank)], writes=[("u", jt)])
                    self.linear_A("s5in", KT, 0, D, lambda k: xnT[:, k, :], "xnT", evac, wbuf, next_slot)
                    ctx.barrier()
            yscr = self.yscr.ap()
            with ExitStack() as st:
                sb = lambda n, shp, dt=F32: st.enter_context(nc.sbuf_tensor("s5_" + n, shp, dt))
                iota = sb("iota", [128, TOK])
                iota1 = sb("iota1", [128, TOK])
                msk = sb("msk", [128, 16])
                prm = sb("prm", [128, 40, GP])
                Yb = sb("Yb", [128, 2, GP, 32], BF16)
                Lst = sb("Lst", [128, 2, 4, 128], BF16)
                Lb = sb("Lb", [128, 2, 4, 128], BF16)
                Lc = sb("Lc", [128, 2, 128], BF16)
                cnat = sb("cnat", [128, 2, 64])
                Z = sb("Z", [128, 128])
                ki = sb("ki", [128, TOK], I32); fr = sb("fr", [128, TOK])
                wend = sb("wend", [128, 2, GP])
                wall = sb("wall", [128, NCORES, 2, GP])
                sel = sb("sel", [128, NCORES * 3])
                hp = sb("hp", [128, 1])
                stS = st.enter_context(ExitStack())
                sbS = lambda n, shp, dt=F32: stS.enter_context(nc.sbuf_tensor("s5_" + n, shp, dt))
                bre = sbS("bre", [128, GP, 16]); bim = sbS("bim", [128, GP, 16])
                bbr = sbS("bbr", [128, GP, 16]); bbi = sbS("bbi", [128, GP, 16])
                tmpb = sbS("tmpb", [128, GP, 16])
                ctx.op("dve", lambda e: e.memset(hp[:, :], math.pi / 2), writes=["hp"])
                ctx.dma("sp", iota[:, :], self.cst["iota"], writes=["iota"])
                ctx.dma("sp", iota1[:, :], self.cst["iota1"], writes=["iota1"])
                ctx.dma("sp", msk[:, :], self.cst["msk"], writes=["msk"])
                ctx.dma("sp", sel[:, :], self.cst["sel"], writes=["sel"])
                ctx.dma("sp", prm[:, 0, :], S["a_re"].rearrange("(gp q) -> q gp", q=128), writes=["are"], slow=True)
                ctx.dma("sp", prm[:, 1, :], S["a_im"].rearrange("(gp q) -> q gp", q=128), writes=["aim"], slow=True)
                ld = S["log_dt"].rearrange("(gp g2) -> g2 gp", g2=2)
                for g2 in range(2):
                    ctx.dma("sp", prm[g2 * 64:(g2 + 1) * 64, 2, :], ld[g2:g2 + 1, :].broadcast_to([64, GP]),
                            writes=[("ldt", g2)], slow=True)
                ctx.dma("sp", bre[:, :, :], S["b_re"].rearrange("(gp q h) -> q gp h", q=128, h=16), writes=["bre"])
                ctx.dma("sp", bim[:, :, :], S["b_im"].rearrange("(gp q h) -> q gp h", q=128, h=16), writes=["bim"])
                ARE, AIM, LDT, DT, LNR, PH, MAG, SNP, CSP, ABR, ABI, DEN, FRE, FIM, TA, TB, TC, C1023, S1023, \
                    L1R, L1I, VRE, VIM, TD = range(24)
                pv = lambda i_: prm[:, i_, :]

                def dve(fn, reads, writes):
                    return ctx.op("dve", fn, reads=reads, writes=writes)

                def tt_(o, a, b, op):
                    dve(lambda e: e.tensor_tensor(out=pv(o), in0=pv(a), in1=pv(b), op=op),
                        [("p", a), ("p", b)], [("p", o)])

                def ts_(o, a, s1, s2=None, op0=ALU.mult, op1=None):
                    dve(lambda e: e.tensor_scalar(out=pv(o), in0=pv(a), scalar1=s1, scalar2=s2, op0=op0,
                                                  **({"op1": op1} if op1 is not None else {})),
                        [("p", a)], [("p", o)])
                for nm, idx in (("are", ARE), ("aim", AIM)):
                    ctx.last_w[("p", idx)] = ctx.last_w[nm]
                ctx.op("act", lambda e: e.activation(out=pv(DT), in_=pv(LDT), func=AF.Exp),
                       reads=[("ldt", 0), ("ldt", 1)], writes=[("p", DT)])
                tt_(LNR, DT, ARE, ALU.mult)
                tt_(PH, DT, AIM, ALU.mult)
                ts_(PH, PH, 1.0 / TWO_PI)
                ctx.op("act", lambda e: e.activation(out=pv(MAG), in_=pv(LNR), func=AF.Exp),
                       reads=[("p", LNR)], writes=[("p", MAG)])

                def small_sincos(turn_idx, mult, sn_idx, cs_idx, tag):
                    def tf(e, out, add):
                        return e.tensor_scalar(out=out, in0=pv(turn_idx), scalar1=float(mult), scalar2=float(add),
                                               op0=ALU.mult, op1=ALU.add)
                    ctx.last_w[tag + "_in"] = ctx.last_w[("p", turn_idx)]
                    self.sincos(tf, GP, pv(sn_idx), None, ki, fr, hp, tag)
                    ctx.last_w[("p", sn_idx)] = ctx.last_w[tag + "_out"]
                def small_sc(turn_idx, mult, sn_idx, cs_idx, tag):
                    def tf(e, out, add):
                        return e.tensor_scalar(out=out, in0=pv(turn_idx), scalar1=float(mult), scalar2=float(add),
                                               op0=ALU.mult, op1=ALU.add)
                    ctx.last_w[tag + "_in"] = ctx.last_w[("p", turn_idx)]
                    self.sincos(tf, GP, pv(sn_idx), pv(cs_idx), ki, fr, hp, tag)
                    ctx.last_w[("p", sn_idx)] = ctx.last_w[tag + "_out"]
                    ctx.last_w[("p", cs_idx)] = ctx.last_w[tag + "_out"]
                small_sc(PH, 1.0, SNP, CSP, "sc1")
                tt_(ABR, MAG, CSP, ALU.mult)
                tt_(ABI, MAG, SNP, ALU.mult)
                tt_(TA, ARE, ARE, ALU.mult)
                tt_(TB, AIM, AIM, ALU.mult)
                tt_(DEN, TA, TB, ALU.add)
                dve(lambda e: e.reciprocal(out=pv(DEN), in_=pv(DEN)), [("p", DEN)], [("p", DEN)])
                ts_(TC, ABR, -1.0, None, op0=ALU.add)
                tt_(TA, TC, ARE, ALU.mult)
                tt_(TB, ABI, AIM, ALU.mult)
                tt_(FRE, TA, TB, ALU.add)
                tt_(FRE, FRE, DEN, ALU.mult)
                tt_(TA, ABI, ARE, ALU.mult)
                tt_(TB, TC, AIM, ALU.mult)
                tt_(FIM, TA, TB, ALU.subtract)
                tt_(FIM, FIM, DEN, ALU.mult)
                fb = lambda i_: prm[:, i_, :].unsqueeze(2).broadcast_to([128, GP, 16])

                def bb_(o, a, fi, op_):
                    pass
                dve(lambda e: e.tensor_tensor(out=bbr[:, :, :], in0=bre[:, :, :], in1=fb(FRE), op=ALU.mult),
                    ["bre", ("p", FRE)], ["bbr"])
                dve(lambda e: e.tensor_tensor(out=tmpb[:, :, :], in0=bim[:, :, :], in1=fb(FIM), op=ALU.mult),
                    ["bim", ("p", FIM)], ["tmpb"])
                dve(lambda e: e.tensor_tensor(out=bbr[:, :, :], in0=bbr[:, :, :], in1=tmpb[:, :, :], op=ALU.subtract),
                    ["bbr", "tmpb"], ["bbr"])
                dve(lambda e: e.tensor_tensor(out=bbi[:, :, :], in0=bim[:, :, :], in1=fb(FRE), op=ALU.mult),
                    ["bim", ("p", FRE)], ["bbi"])
                dve(lambda e: e.tensor_tensor(out=tmpb[:, :, :], in0=bre[:, :, :], in1=fb(FIM), op=ALU.mult),
                    ["bre", ("p", FIM)], ["tmpb"])
                dve(lambda e: e.tensor_tensor(out=bbi[:, :, :], in0=bbi[:, :, :], in1=tmpb[:, :, :], op=ALU.add),
                    ["bbi", "tmpb"], ["bbi"])
                dve(lambda e: e.memset(Yb[:, :, :, :], 0.0), [], ["Yb"])
                for c_, src in ((0, bbr), (1, bbi)):
                    for g2 in range(2):
                        dve(lambda e, c_=c_, src=src, g2=g2: e.tensor_copy(
                            out=Yb[g2 * 64:(g2 + 1) * 64, c_, :, g2 * 16:(g2 + 1) * 16],
                            in_=src[g2 * 64:(g2 + 1) * 64, :, :]), ["bbr", "bbi", "Yb"], ["Yb"])
                ctx.barrier()
                stS.close()
                Ct = sb("Ct", [128, TOK]); Sn = sb("Sn", [128, TOK])
                t1 = sb("t1", [128, TOK]); t2 = sb("t2", [128, TOK])
                ere = sb("ere", [128, TOK]); eim = sb("eim", [128, TOK])
                wre = sb("wre", [128, TOK]); wim = sb("wim", [128, TOK])
                P = sb("P", [128, 4, TOK], BF16)
                ysb = sb("ysb", [128, TOK])
                for ct in range(KT):
                    for c_ in range(2):
                        bank = 6 + c_
                        pTb = self.ps[bank][:, :].bitcast(BF16)
                        ctx.mm([lambda e, c_=c_, pTb=pTb, ct=ct: e.transpose(
                            out=pTb[:, 0:128],
                            in_=Yb[:, c_, ct * 4:ct * 4 + 4, :].rearrange("q a b -> q (a b)"),
                            identity=self.ident[:, :])],
                            reads=["Yb", "ident"], writes=[("ps", bank)])
                        for pp in range(4):
                            dve(lambda e, c_=c_, pp=pp, pTb=pTb: e.tensor_scalar(
                                out=Lb[:, c_, pp, :], in0=pTb[:, 0:128],
                                scalar1=msk[:, 8 + pp:9 + pp], scalar2=None, op0=ALU.mult),
                                [("ps", bank), "msk"], [("Lb", c_, pp)])
                    self.s5_build_C(ct, S, cnat, Z, msk, Lst)
                    for pp in range(4):
                        gp = ct * 4 + pp
                        self.s5_pair_main(ct, pp, gp, uz, Lb, Lst, prm, PH, MAG, iota, Ct, Sn, ki, fr, hp,
                                          t1, t2, ere, eim, wre, wim, P, wend)
                    for hf in range(2):
                        dve(lambda e, hf=hf, ct=ct: e.scalar_tensor_tensor(
                            out=ysb[:, hf * 512:(hf + 1) * 512], in0=uz[:, ct, hf * 512:(hf + 1) * 512],
                            scalar=dcol[:, ct:ct + 1], in1=self.ps[4 + hf][:, :], op0=ALU.mult, op1=ALU.add),
                            [("ps", 4 + hf), ("u", ct), "dcol"], [("ysb", hf)])
                    ctx.dma("sp", yscr[ct * 128:(ct + 1) * 128, :], ysb[:, :], reads=[("ysb", 0), ("ysb", 1)],
                            writes=[("yscr", ct)])
                ctx.barrier()
                self.s5_carry(prm, wend, wall, sel, ki, fr, hp, iota,
                              dict(PH=PH, LNR=LNR, C1023=C1023, S1023=S1023, L1R=L1R, L1I=L1I, VRE=VRE,
                                   VIM=VIM, TA=TA, TB=TB, TC=TC, TD=TD, SNP=SNP, CSP=CSP, MAG=MAG), GP)
                for ct in range(KT):
                    self.s5_build_C(ct, S, cnat, Z, msk, Lst)
                    for pp in range(4):
                        gp = ct * 4 + pp
                        self.s5_pair_corr(ct, pp, gp, Lst, Lc, prm, PH, LNR, VRE, VIM, iota, iota1, Ct, Sn, ki, fr,
                                          hp, t1, t2, P)
                    ctx.dma("sp", ysb[:, :], yscr[ct * 128:(ct + 1) * 128, :], reads=[("yscr", ct)],
                            writes=[("ysb", 0), ("ysb", 1)])
                    T1 = [("t1", 0), ("t1", 1)]
                    T2 = [("t2", 0), ("t2", 1)]
                    YS = [("ysb", 0), ("ysb", 1)]
                    for hf in range(2):
                        hs = slice(hf * 512, (hf + 1) * 512)
                        dve(lambda e, hf=hf, hs=hs: e.tensor_tensor(out=ysb[:, hs], in0=ysb[:, hs],
                                                                    in1=self.ps[4 + hf][:, :], op=ALU.add),
                            [("ysb", hf), ("ps", 4 + hf)], [("ysb", hf)])
                    dve(lambda e: e.tensor_tensor(out=t1[:, :], in0=ysb[:, :], in1=ysb[:, :], op=ALU.mult), YS, T1)
                    dve(lambda e: e.tensor_scalar(out=t1[:, :], in0=t1[:, :], scalar1=0.044715, scalar2=1.0,
                                                  op0=ALU.mult, op1=ALU.add), T1, T1)
                    dve(lambda e: e.tensor_tensor(out=t1[:, :], in0=t1[:, :], in1=ysb[:, :], op=ALU.mult),
                        T1 + YS, T1)
                    ctx.op("act", lambda e: e.activation(out=t2[:, :], in_=t1[:, :], func=AF.Sigmoid,
                                                         scale=2.0 * math.sqrt(2.0 / math.pi)),
                           reads=T1, writes=T2)
                    dve(lambda e, ct=ct: e.tensor_tensor(out=uz[:, ct, :], in0=t2[:, :], in1=ysb[:, :], op=ALU.mult),
                        T2 + YS, [("z", ct)])
                ctx.barrier()
            with ExitStack() as st:
                zz = st.enter_context(nc.sbuf_tensor("zz", [128, KT, TT], BF16))
                wbuf = st.enter_context(nc.sbuf_tensor("wbuf", [128, NW, 8, 512], BF16))
                f_sbt = st.enter_context(nc.sbuf_tensor("f_sb", [128, 4 * D], F32))
                f_sb = f_sbt[:, :].rearrange("p (s d) -> p s d", d=D)
                sg = st.enter_context(nc.sbuf_tensor("sg", [128, 4, TT], F32))
                hc = st.enter_context(nc.sbuf_tensor("hc", [128, 2, 1024], F32))
                gc = st.enter_context(nc.sbuf_tensor("gc", [128, 1024], F32))
                for tt in range(TOK // TT):
                    tsl = slice(tt * TT, (tt + 1) * TT)

                    def evac(jt, bank, tsl=tsl):
                        ctx.op("act", lambda e: e.activation(out=sg[:, jt % 4, :], in_=self.ps[bank][:, :],
                                                             func=AF.Sigmoid, bias=bglu[:, jt:jt + 1]),
                               reads=[("ps", bank), "bglu"], writes=[("sg", jt % 4)])
                        ctx.op("dve", lambda e: e.tensor_tensor(out=zz[:, jt, :], in0=sg[:, jt % 4, :],
                                                                in1=uz[:, jt, tsl], op=ALU.mult),
                               reads=[("sg", jt % 4)], writes=[("zz", jt)])
                    self.linear_A("s5glu", KT, 0, D, lambda k, tsl=tsl: uz[:, k, tsl], "z", evac, wbuf, next_slot)

                    def evac2(n, ts, bank, cw):
                        ctx.op("dve", lambda e: e.tensor_copy(out=f_sb[:, ts, n * 512:n * 512 + cw],
                                                              in_=self.ps[bank][:, 0:cw]),
                               reads=[("ps", bank)], writes=[("f", ts, n)])
                        ctx.op("act", lambda e: e.activation(
                            out=sg[:, 0, 0:cw], in_=f_sb[:, ts, n * 512:n * 512 + cw], func=AF.Square,
                            accum_out=small[:, 8 + ts * 8 + n:9 + ts * 8 + n]),
                            reads=[("f", ts, n)], writes=[("sg", 0), ("ssq", ts, n)])
                    self.linear_B("s5out", KT, 0, D, lambda k, ts: zz[:, k, ts * 128:(ts + 1) * 128], "zz",
                                  evac2, wbuf, next_slot)
                    self.postnorm_residual(h_src, h_dst, tt, gidx + 1, 1.0, f_sb, small, hc, gc)
                    ctx.barrier()

    def s5_build_C(self, ct, S, cnat, Z, msk, Lst):
        ctx = self.ctx
        dve = lambda fn, r, w: ctx.op("dve", fn, reads=r, writes=w)
        ctx.dma("sp", cnat[:, 0, :], S["c_re"].rearrange("(r p) -> r p", p=64)[ct * 128:(ct + 1) * 128, :],
                writes=[("cnat", 0)])
        ctx.dma("sp", cnat[:, 1, :], S["c_im"].rearrange("(r p) -> r p", p=64)[ct * 128:(ct + 1) * 128, :],
                writes=[("cnat", 1)])
        for pp in range(4):
            for c_ in range(2):
                sgn = 1.0 if c_ == 0 else -1.0
                for g2 in range(2):
                    dve(lambda e, c_=c_, g2=g2, pp=pp, sgn=sgn: e.tensor_scalar(
                        out=Z[:, g2 * 64:(g2 + 1) * 64], in0=cnat[:, c_, :],
                        scalar1=msk[:, 2 * pp + g2:2 * pp + g2 + 1], scalar2=sgn,
                        op0=ALU.mult, op1=ALU.mult),
                        [("cnat", c_), "msk"], [("Z", g2)])
                bank = 6 + (c_ % 2)
                ctx.mm([lambda e, bank=bank: e.transpose(out=self.ps[bank][:, 0:128], in_=Z[:, :],
                                                          identity=self.identf[:, :])],
                       reads=[("Z", 0), ("Z", 1), "identf"], writes=[("ps", bank)])
                ctx.op("act", lambda e, c_=c_, pp=pp, bank=bank: e.activation(
                    out=Lst[:, c_, pp, :], in_=self.ps[bank][:, 0:128], func=AF.Copy),
                    reads=[("ps", bank)], writes=[("Lst", c_, pp)])

    def s5_tables(self, gp, prm, PH, iota, Ct, Sn, ki, fr, hp):
        ctx = self.ctx
        ph = prm[:, PH, gp:gp + 1]

        def tf(e, out, add):
            return e.tensor_scalar(out=out, in0=iota[:, :], scalar1=ph, scalar2=float(add),
                                   op0=ALU.mult, op1=ALU.add)
        ctx.last_w["tb_in"] = ctx.last_w.get(("p", PH))
        self.sincos(tf, TOK, Sn[:, :], Ct[:, :], ki, fr, hp, "tb")

    def s5_pair_main(self, ct, pp, gp, uz, Lb, Lst, prm, PH, MAG, iota, Ct, Sn, ki, fr, hp,
                     t1, t2, ere, eim, wre, wim, P, wend):
        ctx = self.ctx
        dve = lambda fn, r, w: ctx.op("dve", fn, reads=r, writes=w)
        for c_ in range(2):
            for hf in range(2):
                bank = 2 * c_ + hf
                ctx.mm([lambda e, c_=c_, hf=hf, bank=bank: e.matmul(
                    self.ps[bank][:, :], lhsT=Lb[:, c_, pp, :], rhs=uz[:, ct, hf * 512:(hf + 1) * 512],
                    start=True, stop=True)],
                    reads=[("Lb", c_, pp), ("u", ct)], writes=[("ps", bank)])
        self.s5_tables(gp, prm, PH, iota, Ct, Sn, ki, fr, hp)
        T = "tb_out"
        for hf in range(2):
            hs = slice(hf * 512, (hf + 1) * 512)
            bre_, bim_ = self.ps[hf][:, :], self.ps[2 + hf][:, :]
            dve(lambda e, hs=hs, bre_=bre_: e.tensor_tensor(out=t1[:, hs], in0=Ct[:, hs], in1=bre_, op=ALU.mult),
                [T, ("ps", hf)], [("t1", hf)])
            dve(lambda e, hs=hs, bim_=bim_: e.tensor_tensor(out=t2[:, hs], in0=Sn[:, hs], in1=bim_, op=ALU.mult),
                [T, ("ps", 2 + hf)], [("t2", hf)])
            dve(lambda e, hs=hs: e.tensor_tensor(out=ere[:, hs], in0=t1[:, hs], in1=t2[:, hs], op=ALU.add),
                [("t1", hf), ("t2", hf)], [("ere", hf)])
            dve(lambda e, hs=hs, bim_=bim_: e.tensor_tensor(out=t1[:, hs], in0=Ct[:, hs], in1=bim_, op=ALU.mult),
                [T, ("ps", 2 + hf)], [("t1", hf)])
            dve(lambda e, hs=hs, bre_=bre_: e.tensor_tensor(out=t2[:, hs], in0=Sn[:, hs], in1=bre_, op=ALU.mult),
                [T, ("ps", hf)], [("t2", hf)])
            dve(lambda e, hs=hs: e.tensor_tensor(out=eim[:, hs], in0=t1[:, hs], in1=t2[:, hs], op=ALU.subtract),
                [("t1", hf), ("t2", hf)], [("eim", hf)])
        rb = prm[:, MAG, gp:gp + 1].broadcast_to([128, TOK])
        dve(lambda e: e.tensor_tensor_scan(out=wre[:, :], data0=rb, data1=ere[:, :], initial=0.0,
                                           op0=ALU.mult, op1=ALU.add),
            [("ere", 0), ("ere", 1), ("p", MAG)], ["wre"])
        dve(lambda e: e.tensor_tensor_scan(out=wim[:, :], data0=rb, data1=eim[:, :], initial=0.0,
                                           op0=ALU.mult, op1=ALU.add),
            [("eim", 0), ("eim", 1), ("p", MAG)], ["wim"])
        dve(lambda e: e.tensor_copy(out=wend[:, 0, gp:gp + 1], in_=wre[:, TOK - 1:TOK]), ["wre"], [("wend", gp, 0)])
        dve(lambda e: e.tensor_copy(out=wend[:, 1, gp:gp + 1], in_=wim[:, TOK - 1:TOK]), ["wim"], [("wend", gp, 1)])
        for i_, (tab, w_, wn, sg_) in enumerate(((Ct, wre, "wre", 1.0), (Sn, wim, "wim", -1.0),
                                                 (Sn, wre, "wre", 1.0), (Ct, wim, "wim", 1.0))):
            dve(lambda e, i_=i_, tab=tab, w_=w_, sg_=sg_: e.scalar_tensor_tensor(
                out=P[:, i_, :], in0=tab[:, :], scalar=sg_, in1=w_[:, :], op0=ALU.mult, op1=ALU.mult),
                [T, wn], [("P", i_)])
        for hf in range(2):
            hs = slice(hf * 512, (hf + 1) * 512)
            fns = []
            for i_, li in enumerate((0, 0, 1, 1)):
                fns.append(lambda e, i_=i_, li=li, hs=hs, hf=hf: e.matmul(
                    self.ps[4 + hf][:, :], lhsT=Lst[:, li, pp, :], rhs=P[:, i_, hs],
                    start=(pp == 0 and i_ == 0), stop=(pp == 3 and i_ == 3)))
            ctx.mm(fns, reads=[("P", 0), ("P", 1), ("P", 2), ("P", 3), ("Lst", 0, pp), ("Lst", 1, pp)],
                   writes=[("ps", 4 + hf)])

    def s5_pair_corr(self, ct, pp, gp, Lst, Lc, prm, PH, LNR, VRE, VIM, iota, iota1, Ct, Sn, ki, fr, hp, t1, t2, P):
        ctx = self.ctx
        dve = lambda fn, r, w: ctx.op("dve", fn, reads=r, writes=w)
        self.s5_tables(gp, prm, PH, iota, Ct, Sn, ki, fr, hp)
        T = "tb_out"
        ctx.op("act", lambda e: e.activation(out=t1[:, :], in_=iota1[:, :], func=AF.Exp,
                                             scale=prm[:, LNR, gp:gp + 1]),
               reads=["iota1", ("p", LNR)], writes=[("t1", 0), ("t1", 1)])
        dve(lambda e: e.tensor_tensor(out=P[:, 0, :], in0=Ct[:, :], in1=t1[:, :], op=ALU.mult), [T, ("t1", 0), ("t1", 1)], [("P", 0)])
        dve(lambda e: e.tensor_tensor(out=P[:, 1, :], in0=Sn[:, :], in1=t1[:, :], op=ALU.mult), [T, ("t1", 0), ("t1", 1)], [("P", 1)])
        vre, vim = prm[:, VRE, gp:gp + 1], prm[:, VIM, gp:gp + 1]
        dve(lambda e: e.tensor_scalar(out=Lc[:, 0, :], in0=Lst[:, 0, pp, :], scalar1=vre, scalar2=None, op0=ALU.mult),
            [("p", VRE), ("Lst", 0, pp)], [("Lc", 0)])
        dve(lambda e: e.scalar_tensor_tensor(out=Lc[:, 0, :], in0=Lst[:, 1, pp, :], scalar=vim, in1=Lc[:, 0, :],
                                             op0=ALU.mult, op1=ALU.add),
            [("p", VIM), ("Lst", 1, pp), ("Lc", 0)], [("Lc", 0)])
        dve(lambda e: e.tensor_scalar(out=Lc[:, 1, :], in0=Lst[:, 0, pp, :], scalar1=vim, scalar2=-1.0,
                                      op0=ALU.mult, op1=ALU.mult),
            [("p", VIM), ("Lst", 0, pp)], [("Lc", 1)])
        dve(lambda e: e.scalar_tensor_tensor(out=Lc[:, 1, :], in0=Lst[:, 1, pp, :], scalar=vre, in1=Lc[:, 1, :],
                                             op0=ALU.mult, op1=ALU.add),
            [("p", VRE), ("Lst", 1, pp), ("Lc", 1)], [("Lc", 1)])
        for hf in range(2):
            hs = slice(hf * 512, (hf + 1) * 512)
            ctx.mm([lambda e, i_=i_, hs=hs, hf=hf: e.matmul(
                self.ps[4 + hf][:, :], lhsT=Lc[:, i_, :], rhs=P[:, i_, hs],
                start=(pp == 0 and i_ == 0), stop=(pp == 3 and i_ == 1)) for i_ in range(2)],
                reads=[("P", 0), ("P", 1), ("Lc", 0), ("Lc", 1)], writes=[("ps", 4 + hf)])

    def s5_carry(self, prm, wend, wall, sel, ki, fr, hp, iota, I, GP):
        ctx, nc = self.ctx, self.nc
        pv = lambda i_: prm[:, i_, :]
        dve = lambda fn, r, w: ctx.op("dve", fn, reads=r, writes=w)
        X = {n: 24 + i for i, n in enumerate(["A0R", "A0I", "A1R", "A1I", "A2R", "A2I", "L2R", "L2I", "SR", "SI",
                                              "U1", "U2", "M1", "C1K", "S1K"])}

        def tt_(o, a, b, op):
            dve(lambda e: e.tensor_tensor(out=pv(o), in0=pv(a), in1=pv(b), op=op), [("p", a), ("p", b)], [("p", o)])

        def sc(turn_idx, mult, sn_idx, cs_idx, tag):
            def tf(e, out, add):
                return e.tensor_scalar(out=out, in0=pv(turn_idx), scalar1=float(mult), scalar2=float(add),
                                       op0=ALU.mult, op1=ALU.add)
            ctx.last_w[tag + "_in"] = ctx.last_w.get(("p", turn_idx))
            self.sincos(tf, GP, pv(sn_idx), pv(cs_idx), ki, fr, hp, tag)
            ctx.last_w[("p", sn_idx)] = ctx.last_w[tag + "_out"]
            ctx.last_w[("p", cs_idx)] = ctx.last_w[tag + "_out"]
        sc(I["PH"], float(TOK - 1), I["S1023"], I["C1023"], "sc2")
        wr, wi = wend[:, 0, :], wend[:, 1, :]
        allw = [("wend", g, c) for g in range(GP) for c in range(2)]
        dve(lambda e: e.tensor_tensor(out=pv(X["U1"]), in0=pv(I["C1023"]), in1=wr, op=ALU.mult), allw + [("p", I["C1023"])], [("p", X["U1"])])
        dve(lambda e: e.tensor_tensor(out=pv(X["U2"]), in0=pv(I["S1023"]), in1=wi, op=ALU.mult), allw + [("p", I["S1023"])], [("p", X["U2"])])
        tt_(X["SR"], X["U1"], X["U2"], ALU.subtract)
        dve(lambda e: e.tensor_tensor(out=pv(X["U1"]), in0=pv(I["S1023"]), in1=wr, op=ALU.mult), allw + [("p", I["S1023"])], [("p", X["U1"])])
        dve(lambda e: e.tensor_tensor(out=pv(X["U2"]), in0=pv(I["C1023"]), in1=wi, op=ALU.mult), allw + [("p", I["C1023"])], [("p", X["U2"])])
        tt_(X["SI"], X["U1"], X["U2"], ALU.add)
        dve(lambda e: e.tensor_copy(out=wend[:, 0, :], in_=pv(X["SR"])), [("p", X["SR"])], ["wendT"])
        dve(lambda e: e.tensor_copy(out=wend[:, 1, :], in_=pv(X["SI"])), [("p", X["SI"])], ["wendT"])
        if self.use_cc:
            ctx.dma("sp", self.wend_b.ap(), wend[:, :, :].rearrange("q c g -> q (c g)"), reads=["wendT"], writes=["wend_b"])
            ctx.collective(self.wend_b, self.wend_all, reads=["wend_b"], writes=["wend_all"])
            ctx.dma("sp", wall[:, :, :, :].rearrange("q r c g -> q r (c g)"),
                    self.wend_all.ap().rearrange("(r q) x -> q r x", q=128), reads=["wend_all"], writes=["wall"])
        else:
            dve(lambda e: e.memset(wall[:, :, :, :], 0.0), [], ["wall"])
        for d in range(3):
            for c_ in range(2):
                o = X["A%d%s" % (d, "RI"[c_])]
                dve(lambda e, o=o: e.memset(pv(o), 0.0), [], [("p", o)])
                for r in range(NCORES):
                    dve(lambda e, o=o, r=r, c_=c_, d=d: e.scalar_tensor_tensor(
                        out=pv(o), in0=wall[:, r, c_, :], scalar=sel[:, 3 * r + d:3 * r + d + 1], in1=pv(o),
                        op0=ALU.mult, op1=ALU.add), ["wall", "sel", ("p", o)], [("p", o)])
        dve(lambda e: e.tensor_scalar(out=pv(X["U1"]), in0=pv(I["LNR"]), scalar1=float(TOK), scalar2=None, op0=ALU.mult),
            [("p", I["LNR"])], [("p", X["U1"])])
        ctx.op("act", lambda e: e.activation(out=pv(X["M1"]), in_=pv(X["U1"]), func=AF.Exp),
               reads=[("p", X["U1"])], writes=[("p", X["M1"])])
        sc(I["PH"], float(TOK), X["S1K"], X["C1K"], "sc3")
        tt_(I["L1R"], X["M1"], X["C1K"], ALU.mult)
        tt_(I["L1I"], X["M1"], X["S1K"], ALU.mult)
        tt_(X["U1"], I["L1R"], I["L1R"], ALU.mult)
        tt_(X["U2"], I["L1I"], I["L1I"], ALU.mult)
        tt_(X["L2R"], X["U1"], X["U2"], ALU.subtract)
        tt_(X["U1"], I["L1R"], I["L1I"], ALU.mult)
        tt_(X["L2I"], X["U1"], X["U1"], ALU.add)
        for (lr, li, ar, ai) in ((I["L1R"], I["L1I"], X["A1R"], X["A1I"]), (X["L2R"], X["L2I"], X["A2R"], X["A2I"])):
            tt_(X["U1"], lr, ar, ALU.mult)
            tt_(X["U2"], li, ai, ALU.mult)
            tt_(X["U1"], X["U1"], X["U2"], ALU.subtract)
            tt_(X["A0R"], X["A0R"], X["U1"], ALU.add)
            tt_(X["U1"], lr, ai, ALU.mult)
            tt_(X["U2"], li, ar, ALU.mult)
            tt_(X["U1"], X["U1"], X["U2"], ALU.add)
            tt_(X["A0I"], X["A0I"], X["U1"], ALU.add)
        tt_(X["U1"], I["CSP"], X["A0R"], ALU.mult)
        tt_(X["U2"], I["SNP"], X["A0I"], ALU.mult)
        tt_(I["VRE"], X["U1"], X["U2"], ALU.subtract)
        tt_(X["U1"], I["SNP"], X["A0R"], ALU.mult)
        tt_(X["U2"], I["CSP"], X["A0I"], ALU.mult)
        tt_(I["VIM"], X["U1"], X["U2"], ALU.add)


    def attn(self, h_src, h_dst, gidx):
        ctx, cfg, nc = self.ctx, self.cfg, self.nc
        D, KT, NH, NKV, QPK = cfg.D, cfg.KT, cfg.NH, cfg.NKV, cfg.QPK
        KVW = cfg.KVW
        from contextlib import ExitStack
        NW = 3
        A = self.attin
        dve = lambda fn, r, w: ctx.op("dve", fn, reads=r, writes=w)
        act = lambda fn, r, w: ctx.op("act", fn, reads=r, writes=w)
        NB = TOK // 128
        HW = NKV * 128 + KVW
        slot = [0]

        def next_slot():
            s_ = slot[0]
            slot[0] = (s_ + 1) % NW
            return s_
        with ExitStack() as st0:
            sb0 = lambda n, shp, dt=F32: st0.enter_context(nc.sbuf_tensor("as_" + n, shp, dt))
            qT = sb0("qT", [128, KT, TOK], BF16)
            small = sb0("small", [128, 64])
            gcol = sb0("gcol", [128, 32])
            dve(lambda e: e.memset(small[:, :], 0.0), [], ["ss", "t1", "t2", "rstd", "tot"])
            with ExitStack() as st1:
                sb1 = lambda n, shp, dt=F32: st1.enter_context(nc.sbuf_tensor("as_" + n, shp, dt))
                kTd = sb1("kTd", [128, NKV, 128 + TOK], BF16)
                vtok = sb1("vtok", [128, NB + 1, KVW], BF16)
                sinkb = sb1("sinkb", [128, NH])
                hsel = sb1("hsel", [128, NCORES])
                maskt = sb1("mask", [128, 2, 256])
                ctx.dma("sp", sinkb[:, :], A["sinks"].rearrange("(o n) -> o n", o=1).broadcast_to([128, NH]),
                        writes=["sinkb"])
                ctx.dma("sp", hsel[:, :], self.cst["hsel"], writes=["hsel"])
                ctx.dma("sp", maskt[:, :, :], self.cst["amask"].rearrange("p (a b) -> p a b", a=2), writes=["mask"])
                with ExitStack() as st:
                    sb = lambda n, shp, dt=F32: st.enter_context(nc.sbuf_tensor("as_" + n, shp, dt))
                    R64 = sb("R64", [128, 7 * D // 2])
                    wbuf = sb("wbuf", [128, NW, 8, 512], BF16)
                    kT = sb("kT", [128, NKV // 2, TOK], BF16)
                    Rb = R64[:, :].bitcast(BF16)
                    xnT = Rb[:, 0:KT * TT].rearrange("p (k t) -> p k t", t=TT)
                    hb = R64[:, 2 * D:3 * D]
                    xb = Rb[:, 6 * D:7 * D]
                    R = {"xnT": xnT, "hb": hb, "xb": xb, "gcol": gcol, "small": small}
                    cosT = sb("cosT", [128, TT]); sinT = sb("sinT", [128, TT])
                    rotf = sb("rotf", [128, 128]); rotb = sb("rotb", [128, 128], BF16)
                    bq = sb("bq", [128, KT + NKV // 2])
                    bv = sb("bv", [128, KVW])
                    qf = sb("qf", [128, 2, TT], BF16)
                    r1 = sb("r1", [128, TT]); r2 = sb("r2", [128, TT])
                    ctx.dma("sp", rotf[:, :], self.cst["rotT"], writes=["rotf"])
                    dve(lambda e: e.tensor_copy(out=rotb[:, :], in_=rotf[:, :]), ["rotf"], ["rotb"])
                    ctx.dma("sp", bq[:, 0:KT + NKV // 2],
                            A["bqkv"][0:D + KVW].rearrange("(k p) -> p k", p=128), writes=["bq"], slow=True)
                    dve(lambda e: e.tensor_scalar(out=bq[:, 0:KT], in0=bq[:, 0:KT], scalar1=0.125, scalar2=None,
                                                  op0=ALU.mult), ["bq"], ["bq"])
                    ctx.dma("sp", bv[:, :], A["bqkv"][D + KVW:D + 2 * KVW].rearrange("(o n) -> o n", o=1)
                            .broadcast_to([128, KVW]), writes=["bv"])
                    for tt in range(TOK // TT):
                        tsl = slice(tt * TT, (tt + 1) * TT)
                        ctx.dma("sp", cosT[:, :], self.cst["cosT"][:, tsl], writes=["cosT"])
                        ctx.dma("sp", sinT[:, :], self.cst["sinT"][:, tsl], writes=["sinT"])
                        self.prenorm(h_src, tt, gidx, R, None)

                        def evac_qk(jt, bank, tt=tt, tsl=tsl, isq=True):
                            bcol = jt if isq else KT + jt
                            sc_ = 0.125 if isq else 1.0
                            i2 = jt % 2
                            act(lambda e: e.activation(out=qf[:, i2, :], in_=self.ps[bank][:, :], func=AF.Identity,
                                                       bias=bq[:, bcol:bcol + 1], scale=sc_),
                                [("ps", bank), "bq"], [("qf", i2)])
                            rb_ = 4 + i2
                            ctx.mm([lambda e: e.matmul(self.ps[rb_][:, :], lhsT=rotb[:, :], rhs=qf[:, i2, :],
                                                       start=True, stop=True)],
                                   reads=["rotb", ("qf", i2)], writes=[("ps", rb_)])
                            dve(lambda e: e.tensor_tensor(out=r1[:, :], in0=qf[:, i2, :], in1=cosT[:, :], op=ALU.mult),
                                [("qf", i2), "cosT"], ["r1"])
                            dve(lambda e: e.tensor_tensor(out=r2[:, :], in0=self.ps[rb_][:, :], in1=sinT[:, :],
                                                          op=ALU.mult), [("ps", rb_), "sinT"], ["r2"])
                            dst = qT[:, jt, tsl] if isq else kT[:, jt, tsl]
                            dve(lambda e: e.tensor_tensor(out=dst, in0=r1[:, :], in1=r2[:, :], op=ALU.add),
                                ["r1", "r2"], [("q" if isq else "k", jt)])
                        self.linear_A("wqkv", KT, 0, D, lambda k: xnT[:, k, :], "xnT", evac_qk, wbuf, next_slot, bw=256)
                        self.linear_A("wqkv", KT, D, KVW, lambda k: xnT[:, k, :], "xnT",
                                      lambda jt, bank: evac_qk(jt, bank, isq=False), wbuf, next_slot, bw=256)

                        def evac_v(n, ts, bank, cw, tt=tt):
                            blk = 1 + tt * 4 + ts
                            dve(lambda e: e.tensor_tensor(out=vtok[:, blk, :], in0=self.ps[bank][:, 0:KVW],
                                                          in1=bv[:, :], op=ALU.add),
                                [("ps", bank), "bv"], [("vtok", blk)])
                        self.linear_B("wqkv", KT, D + KVW, KVW, lambda k, ts: xnT[:, k, ts * 128:(ts + 1) * 128], "xnT",
                                      evac_v, wbuf, next_slot)
                        ctx.barrier()
                    for hk in range(NKV):
                        src = kT[(hk % 2) * 64:(hk % 2 + 1) * 64, hk // 2, :]
                        for half in range(2):
                            ctx.dma("sp", kTd[half * 64:(half + 1) * 64, hk, 128:128 + TOK], src,
                                    reads=[("k", hk // 2)], writes=[("kTd", hk, half)])
                    ctx.barrier()
                with ExitStack() as st:
                    sb = lambda n, shp, dt=F32: st.enter_context(nc.sbuf_tensor("as_" + n, shp, dt))
                    hst = sb("hst", [128, HW])
                    hrx = sb("hrx", [128, 2, HW])
                    hacc = sb("hacc", [128, HW])
                    dve(lambda e: e.tensor_copy(out=hst[:, 0:NKV * 128].rearrange("p (h t) -> p h t", t=128),
                                                in_=kTd[:, :, TOK:TOK + 128]), [], ["hst"])
                    dve(lambda e: e.tensor_copy(out=hst[:, NKV * 128:HW], in_=vtok[:, NB, :]), ["hst"], ["hst"])
                    dve(lambda e: e.memset(hacc[:, :], 0.0), [], ["hacc"])
                    if self.use_cc:
                        ctx.dma("sp", self.halo_b.ap(), hst[:, :], reads=["hst"], writes=["halo_b"])
                        ctx.collective(self.halo_b, self.halo_all, reads=["halo_b"], writes=["halo_all"])
                        hall = self.halo_all.ap().rearrange("(r q) x -> r q x", q=128)
                        for r in range(NCORES):
                            ctx.dma("sp", hrx[:, r % 2, :], hall[r], reads=["halo_all"], writes=[("hrx", r % 2)])
                            dve(lambda e, r=r: e.scalar_tensor_tensor(
                                out=hacc[:, :], in0=hrx[:, r % 2, :], scalar=hsel[:, r:r + 1], in1=hacc[:, :],
                                op0=ALU.mult, op1=ALU.add), [("hrx", r % 2), "hsel", "hacc"], ["hacc"])
                    dve(lambda e: e.tensor_copy(out=kTd[:, :, 0:128],
                                                in_=hacc[:, 0:NKV * 128].rearrange("p (h t) -> p h t", t=128)),
                        ["hacc"], ["kTdh"])
                    dve(lambda e: e.tensor_copy(out=vtok[:, 0, :], in_=hacc[:, NKV * 128:HW]), ["hacc"], [("vtok", 0)])
                    ctx.barrier()
                with ExitStack() as st:
                    sb = lambda n, shp, dt=F32: st.enter_context(nc.sbuf_tensor("as_" + n, shp, dt))
                    Sm = sb("Sm", [128, 2, 256])
                    Pf = sb("Pf", [128, 2, 256])
                    Pn = sb("Pn", [128, 2, 256], BF16)
                    PT = sb("PT", [128, 2, 2, 128], BF16)
                    sm = sb("sm", [128, 2, 8])
                    vpc = sb("vpc", [128, 2, NB + 1, 2, 128], BF16)
                    dve(lambda e: e.memset(vpc[:, :, :, :, :].rearrange("p a b c d -> p (a b c d)"), 0.0), [],
                        [("vpc", 0), ("vpc", 1)])
                    cache = {}
                    nxt = [0]
                    for jt in range(KT):
                        need = sorted({(2 * jt) // QPK, (2 * jt + 1) // QPK})
                        for hk in need:
                            if hk in cache:
                                continue
                            sl_ = nxt[0]
                            nxt[0] = (sl_ + 1) % 2
                            for k_ in [k_ for k_, v_ in cache.items() if v_ == sl_]:
                                del cache[k_]
                            cache[hk] = sl_
                            for e_ in range(2):
                                dve(lambda e, e_=e_, sl_=sl_, hk=hk: e.tensor_copy(
                                    out=vpc[:, sl_, :, e_, e_ * 64:(e_ + 1) * 64],
                                    in_=vtok[:, :, hk * 64:(hk + 1) * 64]), [("vpc", sl_)], [("vpc", sl_)])
                        for b in range(NB):
                            ob = 6 + (b % 2)
                            for e_ in range(2):
                                h_ = 2 * jt + e_
                                hk = h_ // QPK
                                ps_s = self.ps[e_]
                                pr = slice(e_ * 64, (e_ + 1) * 64)
                                ctx.mm([lambda e, e_=e_, jt=jt, b=b, hk=hk, pr=pr, ps_s=ps_s: e.matmul(
                                    ps_s[:, 0:256], lhsT=qT[pr, jt, b * 128:(b + 1) * 128],
                                    rhs=kTd[pr, hk, b * 128:b * 128 + 256], start=True, stop=True)],
                                    reads=[("q", jt), ("q2", jt, b)], writes=[("ps", e_)])
                                mi = 0 if b > 0 else 1
                                dve(lambda e, e_=e_, ps_s=ps_s, mi=mi: e.tensor_tensor(
                                    out=Sm[:, e_, :], in0=ps_s[:, 0:256], in1=maskt[:, mi, :], op=ALU.add),
                                    [("ps", e_), "mask"], [("Sm", e_)])
                                dve(lambda e, e_=e_: e.tensor_reduce(out=sm[:, e_, 0:1], in_=Sm[:, e_, :], op=ALU.max,
                                                                     axis=AX.X), [("Sm", e_)], [("sm0", e_)])
                                dve(lambda e, e_=e_, h_=h_: e.tensor_scalar(
                                    out=sm[:, e_, 1:2], in0=sm[:, e_, 0:1], scalar1=sinkb[:, h_:h_ + 1], scalar2=-1.0,
                                    op0=ALU.max, op1=ALU.mult), [("sm0", e_), "sinkb"], [("sm1", e_)])
                                act(lambda e, e_=e_: e.activation(out=Pf[:, e_, :], in_=Sm[:, e_, :], func=AF.Exp,
                                                                  bias=sm[:, e_, 1:2], accum_out=sm[:, e_, 2:3]),
                                    [("Sm", e_), ("sm1", e_)], [("Pf", e_), ("sm2", e_)])
                                act(lambda e, e_=e_, h_=h_: e.activation(out=sm[:, e_, 3:4], in_=sinkb[:, h_:h_ + 1],
                                                                         func=AF.Exp, bias=sm[:, e_, 1:2]),
                                    [("sm1", e_), "sinkb"], [("sm3", e_)])
                                dve(lambda e, e_=e_: e.tensor_tensor(out=sm[:, e_, 4:5], in0=sm[:, e_, 2:3],
                                                                     in1=sm[:, e_, 3:4], op=ALU.add),
                                    [("sm2", e_), ("sm3", e_)], [("sm4", e_)])
                                dve(lambda e, e_=e_: e.reciprocal(out=sm[:, e_, 5:6], in_=sm[:, e_, 4:5]),
                                    [("sm4", e_)], [("sm5", e_)])
                                dve(lambda e, e_=e_: e.tensor_scalar(out=Pn[:, e_, :], in0=Pf[:, e_, :],
                                                                     scalar1=sm[:, e_, 5:6], scalar2=None, op0=ALU.mult),
                                    [("Pf", e_), ("sm5", e_)], [("Pn", e_)])
                                tb = 2 + e_
                                pT = self.ps[tb][:, :].bitcast(BF16)
                                ctx.mm([(lambda e, kb=kb, e_=e_, pT=pT: e.transpose(
                                    out=pT[:, kb * 128:(kb + 1) * 128], in_=Pn[:, e_, kb * 128:(kb + 1) * 128],
                                    identity=self.ident[:, :])) for kb in range(2)],
                                    reads=[("Pn", e_), "ident"], writes=[("ps", tb)])
                                act(lambda e, e_=e_, pT=pT: e.activation(
                                    out=PT[:, e_, :, :].rearrange("p a b -> p (a b)"), in_=pT[:, 0:256], func=AF.Copy),
                                    [("ps", tb)], [("PT", e_)])
                            fns = []
                            rd = [("PT", 0), ("PT", 1)]
                            for e_ in range(2):
                                hk = (2 * jt + e_) // QPK
                                sl_ = cache[hk]
                                rd.append(("vpc", sl_))
                                for kb in range(2):
                                    fns.append(lambda e, e_=e_, kb=kb, sl_=sl_, b=b, ob=ob: e.matmul(
                                        self.ps[ob][:, 0:128], lhsT=vpc[:, sl_, b + kb, e_, :], rhs=PT[:, e_, kb, :],
                                        start=(e_ == 0 and kb == 0), stop=(e_ == 1 and kb == 1)))
                            ctx.mm(fns, reads=rd, writes=[("ps", ob)])
                            act(lambda e, jt=jt, b=b, ob=ob: e.activation(
                                out=qT[:, jt, b * 128:(b + 1) * 128], in_=self.ps[ob][:, 0:128], func=AF.Copy),
                                [("ps", ob)], [("q2", jt, b)])
                    ctx.barrier()
            with ExitStack() as st:
                sb = lambda n, shp, dt=F32: st.enter_context(nc.sbuf_tensor("as_" + n, shp, dt))
                wbuf = sb("wbuf", [128, NW, 8, 512], BF16)
                f_sbt = sb("f_sb", [128, 4 * D])
                f_sb = f_sbt[:, :].rearrange("p (s d) -> p s d", d=D)
                sg = sb("sg", [128, TT])
                hc = sb("hc", [128, 2, 1024]); gc = sb("gc", [128, 1024])
                bo = sb("bo", [128, D])
                ctx.dma("sp", bo[:, :], A["bo"].rearrange("(o n) -> o n", o=1).broadcast_to([128, D]), writes=["bo"])
                for tt in range(TOK // TT):
                    def evac2(n, ts, bank, cw):
                        dve(lambda e: e.tensor_tensor(out=f_sb[:, ts, n * 512:n * 512 + cw], in0=self.ps[bank][:, 0:cw],
                                                      in1=bo[:, n * 512:n * 512 + cw], op=ALU.add),
                            [("ps", bank), "bo"], [("f", ts, n)])
                        act(lambda e: e.activation(out=sg[:, 0:cw], in_=f_sb[:, ts, n * 512:n * 512 + cw], func=AF.Square,
                                                   accum_out=small[:, 8 + ts * 8 + n:9 + ts * 8 + n]),
                            [("f", ts, n)], [("sg", 0), ("ssq", ts, n)])
                    self.linear_B("wo", KT, 0, D,
                                  lambda k, ts, tt=tt: qT[:, k, tt * TT + ts * 128:tt * TT + (ts + 1) * 128], "oT",
                                  evac2, wbuf, next_slot)
                    self.postnorm_residual(h_src, h_dst, tt, gidx + 1, 1.0, f_sb, small, hc, gc)
                    ctx.barrier()


def host_consts(core):
    c = {"ident": np.eye(128, dtype=np.float32)}
    c["iota"] = np.broadcast_to(np.arange(TOK, dtype=np.float32), (128, TOK)).copy()
    c["iota1"] = c["iota"] + 1.0
    msk = np.zeros((128, 16), np.float32)
    rows = np.arange(128)
    for g8 in range(8):
        msk[:, g8] = (rows // 16 == g8)
    for pp in range(4):
        msk[:, 8 + pp] = (rows // 32 == pp)
    c["msk"] = msk
    sel = np.zeros((NCORES, 3), np.float32)
    qpos, b = core % 4, core // 4
    for r in range(NCORES):
        if r // 4 == b and r % 4 < qpos:
            sel[r, qpos - 1 - (r % 4)] = 1.0
    c["sel"] = np.broadcast_to(sel.reshape(1, -1), (128, NCORES * 3)).copy()
    hsel = np.zeros((NCORES,), np.float32)
    if qpos > 0:
        hsel[core - 1] = 1.0
    c["hsel"] = np.broadcast_to(hsel.reshape(1, -1), (128, NCORES)).copy()
    qq = np.arange(128)[:, None]
    kk = np.arange(256)[None, :]
    valid = (kk >= qq + 1) & (kk <= qq + 128)
    m_gen = np.where(valid, 0.0, -1e30).astype(np.float32)
    m_first = np.where(valid & ((kk >= 128) | (qpos > 0)), 0.0, -1e30).astype(np.float32)
    c["amask"] = np.concatenate([m_gen, m_first], axis=1)
    pos = (qpos * TOK + np.arange(TOK)).astype(np.float32)
    inv = (np.float32(500000.0) ** (-(np.arange(0, 16, 2, dtype=np.float32) / np.float32(16)))).astype(np.float32)
    ang = pos[None, :] * inv[:, None]
    cosT = np.ones((128, TOK), np.float32)
    sinT = np.zeros((128, TOK), np.float32)
    rot = np.zeros((128, 128), np.float32)
    for e_ in range(2):
        for d in range(16):
            cosT[e_ * 64 + d] = np.cos(ang[d % 8])
            sinT[e_ * 64 + d] = np.sin(ang[d % 8])
        for d in range(8):
            rot[e_ * 64 + d + 8, e_ * 64 + d] = -1.0
            rot[e_ * 64 + d, e_ * 64 + d + 8] = 1.0
    c["cosT"], c["sinT"], c["rotT"] = cosT, sinT, rot
    return c


def big_for_stages(stages):
    big = []
    for st in stages:
        if st[0] == "ffn":
            big += ["wg%d%d" % (st[1], st[2]), "wu%d%d" % (st[1], st[2]), "wd%d%d" % (st[1], st[2])]
        elif st[0] == "s5":
            big += ["s5in", "s5glu", "s5out"]
        elif st[0] == "attn":
            big += ["wqkv", "wo"]
    return big


def big_source(inp, name):
    if name[0] == "w" and name[1] in "gud" and len(name) == 4:
        key = {"g": "ffn_w_gate", "u": "ffn_w_up", "d": "ffn_w_down"}[name[1]]
        return inp[key][int(name[2]), int(name[3])]
    return {"s5in": lambda: inp["s5_w_in"][0], "s5glu": lambda: inp["s5_w_glu"][0],
            "s5out": lambda: inp["s5_w_out"][0], "wqkv": lambda: inp["attn_w_qkv"][0],
            "wo": lambda: inp["attn_w_o"][0]}[name]()


def make_in_maps(cfg, inp, big):
    D = cfg.D
    x = np.asarray(inp["x"], dtype=np.float32).reshape(NCORES, TOK, D)
    ng = np.ascontiguousarray(np.asarray(inp["norm_g"], dtype=np.float32).reshape(12, D))
    maps = []
    srcs = {n: np.asarray(big_source(inp, n), dtype=np.float32) for n in big}
    for c in range(NCORES):
        m = {"x": x[c], "norm_g": ng}
        m.update(host_consts(c))
        for n, w in srcs.items():
            r = w.shape[0] // NCORES
            m[n] = np.ascontiguousarray(w[c * r:(c + 1) * r])
        for n, k in (("log_dt", "s5_log_dt"), ("a_re", "s5_a_re"), ("a_im", "s5_a_im"), ("b_re", "s5_b_re"),
                     ("b_im", "s5_b_im"), ("c_re", "s5_c_re"), ("c_im", "s5_c_im"), ("d", "s5_d"),
                     ("bglu", "s5_b_glu")):
            m["s5_" + n] = np.ascontiguousarray(np.asarray(inp[k], dtype=np.float32)[0].reshape(-1))
        m["at_bqkv"] = np.ascontiguousarray(np.asarray(inp["attn_b_qkv"], dtype=np.float32)[0])
        m["at_sinks"] = np.ascontiguousarray(np.asarray(inp["attn_sinks"], dtype=np.float32)[0])
        m["at_bo"] = np.ascontiguousarray(np.asarray(inp["attn_b_o"], dtype=np.float32)[0])
        maps.append(m)
    return maps


ALL_STAGES = [("ffn", 0, 0), ("s5",), ("ffn", 0, 1), ("ffn", 1, 0), ("attn",), ("ffn", 1, 1)]


def kernel(**inputs):
    cfg = Cfg(D=4096, FF=11008)
    stages = ALL_STAGES
    big = big_for_stages(stages)
    prog = Prog(cfg, stages, big)
    nc = prog.build()
    in_maps = make_in_maps(cfg, inputs, big)
    res = run_bass_kernel_spmd(nc, in_maps, core_ids=list(range(NCORES)))
    out = np.stack([np.asarray(res.results[c]["out"], dtype=np.float32) for c in range(NCORES)])
    return out.reshape(2, 4096, cfg.D)
```

```python
import math
import numpy as np
import ml_dtypes
import concourse.bass as bass
import concourse.mybir as mybir
from concourse.bass_utils import run_bass_kernel_spmd

F32 = mybir.dt.float32
BF16 = mybir.dt.bfloat16
I32 = mybir.dt.int32
AF = mybir.ActivationFunctionType
ALU = mybir.AluOpType
AX = mybir.AxisListType

NCORES = 8
TOK = 1024
TT = 512
RMS_EPS = 1e-6
TWO_PI = 2.0 * math.pi


class Tok:
    __slots__ = ("sem", "val", "eng")

    def __init__(self, sem, val, eng):
        self.sem, self.val, self.eng = sem, val, eng


class Ctx:
    def __init__(self, nc, stack):
        self.nc = nc
        self.h = {"pe": nc.tensor, "act": nc.scalar, "dve": nc.vector, "pool": nc.gpsimd,
                  "sp": nc.sync}
        self.sem = {}
        self.cnt = {}
        for e in ("pe", "act", "dve", "pool"):
            self.sem[e] = stack.enter_context(nc.semaphore("s_" + e))
            self.cnt[e] = 0
        self.dsem = {}
        self.dcnt = {}
        self.dnext = {}
        for q in ("sp", "pool"):
            self.dsem[q] = [stack.enter_context(nc.semaphore("d_%s%d" % (q, i))) for i in range(8)]
            self.dcnt[q] = [0] * 8
            self.dnext[q] = 0
        self.stack = stack
        self.last_w = {}
        self.readers = {}
        self.waited = {e: {} for e in self.h}
        self.all_toks = {}
        self.streams = {e: [] for e in self.h}

    def _wait(self, eng, tok):
        if tok is None:
            return
        w = self.waited[eng]
        key = id(tok.sem)
        if w.get(key, 0) >= tok.val:
            return
        self.streams[eng].append(lambda h, s_=tok.sem, v=tok.val: h.wait_ge(s_, v))
        w[key] = tok.val

    def _deps(self, eng, reads, writes):
        toks = []
        for r in reads:
            t = self.last_w.get(r)
            if t is not None:
                toks.append(t)
        for w_ in writes:
            t = self.last_w.get(w_)
            if t is not None:
                toks.append(t)
            toks.extend(self.readers.get(w_, ()))
        for t in toks:
            if t.eng == "pe" and eng == "pe":
                continue
            self._wait(eng, t)

    def _commit(self, tok, reads, writes):
        for r in reads:
            self.readers.setdefault(r, []).append(tok)
        for w_ in writes:
            self.last_w[w_] = tok
            self.readers[w_] = []
        self.all_toks[id(tok.sem)] = tok

    def op(self, eng, fn, reads=(), writes=()):
        self._deps(eng, reads, writes)
        self.cnt[eng] += 1
        tok = Tok(self.sem[eng], self.cnt[eng], eng)
        self.streams[eng].append(lambda h, fn=fn, s_=tok.sem: fn(h).then_inc(s_, 1))
        self._commit(tok, reads, writes)
        return tok

    def mm(self, fns, reads=(), writes=()):
        self._deps("pe", reads, writes)
        self.cnt["pe"] += 1
        tok = Tok(self.sem["pe"], self.cnt["pe"], "pe")
        fns = list(fns)
        for fn in fns[:-1]:
            self.streams["pe"].append(fn)
        self.streams["pe"].append(lambda h, fn=fns[-1], s_=tok.sem: fn(h).then_inc(s_, 1))
        self._commit(tok, reads, writes)
        return tok

    def dma(self, q, out, in_, reads=(), writes=(), slow=False):
        self._deps(q, reads, writes)
        i = self.dnext[q]
        self.dnext[q] = (i + 1) % 8
        sem = self.dsem[q][i]
        if self.dcnt[q][i] > 0:
            self._wait(q, Tok(sem, self.dcnt[q][i], "dma"))
        self.dcnt[q][i] += 16
        tok = Tok(sem, self.dcnt[q][i], "dma")
        if slow:
            self.streams[q].append(lambda h: h.dma_start(
                out=out, in_=in_, allow_slow_non_contiguous=True).then_inc(sem, 16))
        else:
            self.streams[q].append(lambda h: h.dma_start(out=out, in_=in_).then_inc(sem, 16))
        self._commit(tok, reads, writes)
        return tok

    def collective(self, src, dst, reads=(), writes=()):
        self._deps("pool", reads, writes)
        sem = self.stack.enter_context(self.nc.semaphore())
        self.streams["pool"].append(lambda h: h.collective_compute(
            "AllGather", ALU.bypass, [list(range(NCORES))],
            ins=[src.ap().opt()], outs=[dst.ap().opt()]).then_inc(sem))
        tok = Tok(sem, 1, "cc")
        self._commit(tok, reads, writes)
        return tok

    def barrier(self):
        toks = list(self.all_toks.values())
        for e in self.h:
            for t in toks:
                self._wait(e, t)
        self.last_w = {}
        self.readers = {}

    def finish(self, tok):
        self._wait("sp", tok)


DEBUG = False


class Cfg:
    def __init__(self, D=4096, FF=11008):
        self.D, self.FF = D, FF
        self.KT = D // 128
        self.KF = FF // 128
        self.G = D // 16
        self.NH = D // 64
        self.NKV = 8
        self.QPK = self.NH // self.NKV
        self.KVW = self.NKV * 64
        self.QKV = D + 2 * self.KVW


BIG = ["wg00", "wu00", "wd00", "s5in", "s5glu", "s5out", "wg01", "wu01", "wd01",
       "wg10", "wu10", "wd10", "wqkv", "wo", "wg11", "wu11", "wd11"]


def big_shapes(cfg):
    D, FF = cfg.D, cfg.FF
    sh = {}
    for i in range(2):
        for s_ in range(2):
            sh["wg%d%d" % (i, s_)] = (D, FF)
            sh["wu%d%d" % (i, s_)] = (D, FF)
            sh["wd%d%d" % (i, s_)] = (FF, D)
    sh["s5in"] = (D, D)
    sh["s5glu"] = (D, D)
    sh["s5out"] = (D, D)
    sh["wqkv"] = (D, cfg.QKV)
    sh["wo"] = (D, D)
    return sh


class _UniqNc:
    def __init__(self, nc):
        object.__setattr__(self, "_nc", nc)
        object.__setattr__(self, "_n", 0)

    def __getattr__(self, k):
        return getattr(self._nc, k)

    def sbuf_tensor(self, name, shape, dtype):
        object.__setattr__(self, "_n", self._n + 1)
        return self._nc.sbuf_tensor("%s_%d" % (name, self._n), shape, dtype)


class Prog:
    def __init__(self, cfg, stages, big_needed, use_cc=True):
        from contextlib import ExitStack
        self.cfg = cfg
        self.stack = ExitStack()
        nc = self.nc = _UniqNc(bass.Bass("TRN2", target_bir_lowering=False))
        D, FF = cfg.D, cfg.FF
        self.x = nc.dram_tensor("x", [TOK, D], F32, kind="ExternalInput").ap()
        self.out = nc.dram_tensor("out", [TOK, D], F32, kind="ExternalOutput").ap()
        self.normg = nc.dram_tensor("norm_g", [12, D], F32, kind="ExternalInput").ap()
        self.ident_in = nc.dram_tensor("ident", [128, 128], F32, kind="ExternalInput").ap()
        sh = big_shapes(cfg)
        self.wsh, self.wfull = {}, {}
        self.use_cc = use_cc
        if use_cc:
            for n in big_needed:
                K, N = sh[n]
                self.wsh[n] = nc.dram_tensor(n, [K // NCORES, N], F32, kind="ExternalInput")
            for n in big_needed:
                K, N = sh[n]
                self.wfull[n] = nc.dram_tensor("F_" + n, [K, N], BF16)
            self.wb = {n: nc.dram_tensor("B_" + n, [sh[n][0] // NCORES, sh[n][1]], BF16) for n in big_needed}
        else:
            for n in big_needed:
                K, N = sh[n]
                self.wfull[n] = nc.dram_tensor(n, [K, N], F32, kind="ExternalInput")
        self.h = nc.dram_tensor("h_scr", [TOK, D], F32)
        self.cst = {}
        for n, shp in (("iota", [128, TOK]), ("iota1", [128, TOK]), ("msk", [128, 16]), ("sel", [128, NCORES * 3])):
            self.cst[n] = nc.dram_tensor(n, shp, F32, kind="ExternalInput").ap()
        G = cfg.G
        self.s5in = {}
        for n, sz in (("log_dt", G), ("a_re", G * 64), ("a_im", G * 64), ("b_re", G * 1024), ("b_im", G * 1024),
                      ("c_re", G * 1024), ("c_im", G * 1024), ("d", D), ("bglu", D)):
            self.s5in[n] = nc.dram_tensor("s5_" + n, [sz], F32, kind="ExternalInput").ap()
        for n, shp in (("hsel", [128, NCORES]), ("amask", [128, 512]), ("cosT", [128, TOK]), ("sinT", [128, TOK]),
                       ("rotT", [128, 128])):
            self.cst[n] = nc.dram_tensor(n, shp, F32, kind="ExternalInput").ap()
        self.attin = {}
        for n, sz in (("bqkv", cfg.QKV), ("sinks", cfg.NH), ("bo", D)):
            self.attin[n] = nc.dram_tensor("at_" + n, [sz], F32, kind="ExternalInput").ap()
        HW_ = cfg.NKV * 128 + cfg.KVW
        self.halo_b = nc.dram_tensor("halo_b", [128, HW_], F32)
        self.halo_all = nc.dram_tensor("halo_all", [NCORES * 128, HW_], F32)
        self.yscr = nc.dram_tensor("yscr", [D, TOK], F32)
        self.wend_b = nc.dram_tensor("wend_b", [128, G], F32)
        self.wend_all = nc.dram_tensor("wend_all", [NCORES * 128, G], F32)
        self.dbg_t = nc.dram_tensor("dbg", [128, 4096], F32, kind="ExternalOutput").ap() if DEBUG else None
        self.dbg_off = 0
        self.big_needed = big_needed
        self.stages = stages

    def build(self):
        nc, cfg = self.nc, self.cfg
        st = self.stack
        with st:
            blk = st.enter_context(nc.Block())
            self.ctx = ctx = Ctx(nc, st)
            self.ps = [st.enter_context(nc.psum_tensor("ps%d" % b, [128, 512], F32)) for b in range(8)]
            self.ident = st.enter_context(nc.sbuf_tensor("identb", [128, 128], BF16))
            self.identf = st.enter_context(nc.sbuf_tensor("identf", [128, 128], F32))
            self.wtok = {}
            if DEBUG:
                self.dbg_stage = st.enter_context(nc.sbuf_tensor("dbg_stage", [128, 512], F32))

            self.emit()
            S = ctx.streams

            @blk.gpsimd
            def _(h):
                for f in S["pool"]:
                    f(h)

            @blk.sync
            def _(h):
                for f in S["sp"]:
                    f(h)

            @blk.tensor
            def _(h):
                for f in S["pe"]:
                    f(h)

            @blk.vector
            def _(h):
                for f in S["dve"]:
                    f(h)

            @blk.scalar
            def _(h):
                for f in S["act"]:
                    f(h)
            print("instr counts", {k: len(v) for k, v in S.items()}, flush=True)
        return nc._nc

    def emit(self):
        ctx, nc, cfg = self.ctx, self.nc, self.cfg
        if self.use_cc:
            names = list(self.big_needed)
            LEAD = 3
            for n in names[:LEAD]:
                ctx.dma("pool", self.wb[n].ap(), self.wsh[n].ap(), writes=[("wb", n)])
            for i, n in enumerate(names):
                self.wtok[n] = ctx.collective(self.wb[n], self.wfull[n], reads=[("wb", n)],
                                              writes=[("wfull", n)])
                if i + LEAD < len(names):
                    m = names[i + LEAD]
                    ctx.dma("pool", self.wb[m].ap(), self.wsh[m].ap(), writes=[("wb", m)])
        ctx.dma("sp", self.identf[:, :], self.ident_in, writes=["identf"])
        ctx.op("dve", lambda e: e.tensor_copy(out=self.ident[:, :], in_=self.identf[:, :]),
               reads=["identf"], writes=["ident"])
        h_src = self.x
        last = None
        for stg in self.stages:
            kind = stg[0]
            if kind == "ffn":
                _, i, s_ = stg
                dst = self.h.ap()
                last = self.ffn(h_src, dst, 6 * i + (0 if s_ == 0 else 4),
                                "wg%d%d" % (i, s_), "wu%d%d" % (i, s_), "wd%d%d" % (i, s_))
                h_src = dst
            elif kind == "attn":
                dst = self.h.ap()
                self.attn(h_src, dst, 8)
                h_src = dst
            elif kind == "s5":
                dst = self.h.ap()
                self.s5(h_src, dst, 2)
                h_src = dst
            ctx.barrier()
        t = ctx.dma("sp", self.out, h_src, reads=["hdram"], writes=["out"])
        ctx.finish(t)

    def prenorm(self, h_src, tt, gidx, R, names):
        ctx, cfg = self.ctx, self.cfg
        D, KT = cfg.D, cfg.KT
        xnT, hb, xb, gcol, small = R["xnT"], R["hb"], R["xb"], R["gcol"], R["small"]
        ctx.dma("sp", gcol[:, 0:KT], self.normg[gidx, :].rearrange("(k p) -> p k", p=128),
                writes=["gcol"], slow=True)
        for ts in range(4):
            r0 = tt * TT + ts * 128
            ctx.dma("sp", hb[:, :], h_src[r0:r0 + 128, :], reads=["hdram"], writes=["hb"])
            ctx.op("act", lambda e: e.activation(out=xb[:, :], in_=hb[:, :], func=AF.Square,
                                                 accum_out=small[:, 0:1]),
                   reads=["hb"], writes=["xb", "ss"])
            ctx.op("dve", lambda e: e.tensor_scalar(out=small[:, 1:2], in0=small[:, 0:1],
                                                    scalar1=1.0 / D, scalar2=RMS_EPS,
                                                    op0=ALU.mult, op1=ALU.add),
                   reads=["ss"], writes=["t1"])
            ctx.op("act", lambda e: e.activation(out=small[:, 2:3], in_=small[:, 1:2], func=AF.Sqrt),
                   reads=["t1"], writes=["t2"])
            ctx.op("dve", lambda e: e.reciprocal(out=small[:, 3:4], in_=small[:, 2:3]),
                   reads=["t2"], writes=["rstd"])
            ctx.op("dve", lambda e: e.tensor_scalar(out=xb[:, :], in0=hb[:, :],
                                                    scalar1=small[:, 3:4], scalar2=None,
                                                    op0=ALU.mult),
                   reads=["hb", "rstd"], writes=["xb"])
            for k0 in range(0, KT, 8):
                kn = min(8, KT - k0)
                bank = 6 + ((k0 // 8) % 2)
                pT = self.ps[bank][:, :].bitcast(BF16)
                ctx.mm([(lambda e, kk=kk, pT=pT, k0=k0: e.transpose(out=pT[:, kk * 128:(kk + 1) * 128],
                                                      in_=xb[:, (k0 + kk) * 128:(k0 + kk + 1) * 128],
                                                      identity=self.ident[:, :]))
                        for kk in range(kn)],
                       reads=["xb", "ident"], writes=[("ps", bank)])
                for kk in range(kn):
                    k = k0 + kk
                    ctx.op("dve", lambda e, kk=kk, k=k, ts=ts, pT=pT: e.tensor_scalar(
                        out=xnT[:, k, ts * 128:(ts + 1) * 128], in0=pT[:, kk * 128:(kk + 1) * 128],
                        scalar1=gcol[:, k:k + 1], scalar2=None, op0=ALU.mult),
                        reads=[("ps", bank), "gcol"], writes=[("xnT", k)])

    def dbg(self, ap, n, nm=None):
        if not DEBUG or (DEBUG is not True and nm not in DEBUG):
            self.dbg_off += n
            return
        self.ctx.barrier()
        print("dbg", self.dbg_off, n)
        self.ctx.op("dve", lambda e: e.tensor_copy(out=self.dbg_stage[:, 0:n], in_=ap), writes=["dbgs"])
        self.ctx.dma("sp", self.dbg_t[:, self.dbg_off:self.dbg_off + n], self.dbg_stage[:, 0:n],
                     reads=["dbgs"], writes=["dbg"])
        self.dbg_off += n
        self.ctx.barrier()

    def wtile(self, name, r0k, kn, c0, cw, wbuf, slot):
        W = self.wfull[name].ap().rearrange("(k p) c -> p k c", p=128)
        return self.ctx.dma("pool", wbuf[:, slot, 0:kn, 0:cw], W[:, r0k:r0k + kn, c0:c0 + cw],
                            reads=[("wfull", name)], writes=[("wbuf", slot)])

    def ffn(self, h_src, h_dst, gidx, wg, wu, wd):
        ctx, cfg, nc = self.ctx, self.cfg, self.nc
        D, FF, KT, KF = cfg.D, cfg.FF, cfg.KT, cfg.KF
        from contextlib import ExitStack
        NW = 3
        with ExitStack() as st:
            act_sb = st.enter_context(nc.sbuf_tensor("act_sb", [128, KF, TT], BF16))
            R64 = st.enter_context(nc.sbuf_tensor("R64", [128, 4 * D], F32))
            wbuf = st.enter_context(nc.sbuf_tensor("wbuf", [128, NW, 8, 512], BF16))
            sg = st.enter_context(nc.sbuf_tensor("sg", [128, 4, TT], F32))
            hc = st.enter_context(nc.sbuf_tensor("hc", [128, 2, 1024], F32))
            gc = st.enter_context(nc.sbuf_tensor("gc", [128, 1024], F32))
            small = st.enter_context(nc.sbuf_tensor("small", [128, 64], F32))
            gcol = st.enter_context(nc.sbuf_tensor("gcol", [128, 32], F32))
            Rb = R64[:, :].bitcast(BF16)
            xnT = Rb[:, 0:KT * TT].rearrange("p (k t) -> p k t", t=TT)
            hb = R64[:, 2 * D:3 * D]
            xb = Rb[:, 6 * D:7 * D]
            f_sb = R64[:, :].rearrange("p (s d) -> p s d", d=D)
            R = {"xnT": xnT, "hb": hb, "xb": xb, "gcol": gcol, "small": small}
            slot = [0]
            ctx.op("dve", lambda e: e.memset(small[:, :], 0.0), writes=["ss", "t1", "t2", "rstd", "tot"])

            def next_slot():
                s_ = slot[0]
                slot[0] = (s_ + 1) % NW
                return s_

            cbs = [(c0, min(512, FF - c0)) for c0 in range(0, FF, 512)]
            last = None
            for tt in range(TOK // TT):
                self.prenorm(h_src, tt, gidx, R, None)
                if tt == 0:
                    self.dbg(hb, 512, "hb")
                    self.dbg(small[:, 0:8], 8, "small")
                    self.dbg(xb, 512, "xb")
                    self.dbg(xnT[:, 0, :], 512, "xnT")
                for (c0, cw) in cbs:
                    nj = cw // 128
                    for which, wname, b0 in (("g", wg, 0), ("u", wu, 4)):
                        for ks in range(0, KT, 8):
                            kn = min(8, KT - ks)
                            sl = next_slot()
                            self.wtile(wname, ks, kn, c0, cw, wbuf, sl)
                            for j in range(nj):
                                ctx.mm([(lambda e, kk=kk, j=j, sl=sl, ks=ks, b0=b0: e.matmul(
                                    self.ps[b0 + j][:, :], lhsT=wbuf[:, sl, kk, j * 128:(j + 1) * 128],
                                    rhs=xnT[:, ks + kk, :], start=(ks + kk == 0),
                                    stop=(ks + kk == KT - 1))) for kk in range(kn)],
                                    reads=[("wbuf", sl)] + [("xnT", ks + kk) for kk in range(kn)],
                                    writes=[("ps", b0 + j)])
                        for j in range(nj):
                            if which == "g":
                                ctx.op("act", lambda e, j=j: e.activation(
                                    out=sg[:, j, :], in_=self.ps[j][:, :], func=AF.Silu),
                                    reads=[("ps", j)], writes=[("sg", j)])
                            else:
                                jt = c0 // 128 + j
                                ctx.op("dve", lambda e, j=j, jt=jt: e.tensor_tensor(
                                    out=act_sb[:, jt, :], in0=sg[:, j, :], in1=self.ps[4 + j][:, :],
                                    op=ALU.mult),
                                    reads=[("sg", j), ("ps", 4 + j)], writes=[("act", jt)])
                if tt == 0:
                    self.dbg(act_sb[:, 0, :], 512, "act")
                for n in range(D // 512):
                    bb = (n % 2) * 4
                    for js in range(0, KF, 8):
                        jn = min(8, KF - js)
                        sl = next_slot()
                        self.wtile(wd, js, jn, n * 512, 512, wbuf, sl)
                        for ts in range(4):
                            ctx.mm([(lambda e, jj=jj, ts=ts, sl=sl, js=js, bb=bb: e.matmul(
                                self.ps[bb + ts][:, :], lhsT=act_sb[:, js + jj, ts * 128:(ts + 1) * 128],
                                rhs=wbuf[:, sl, jj, :], start=(js + jj == 0),
                                stop=(js + jj == KF - 1))) for jj in range(jn)],
                                reads=[("wbuf", sl)] + [("act", js + jj) for jj in range(jn)],
                                writes=[("ps", bb + ts)])
                    for ts in range(4):
                        ctx.op("dve", lambda e, ts=ts, n=n, bb=bb: e.tensor_copy(
                            out=f_sb[:, ts, n * 512:(n + 1) * 512], in_=self.ps[bb + ts][:, :]),
                            reads=[("ps", bb + ts)], writes=[("f", ts, n)])
                        ctx.op("act", lambda e, ts=ts, n=n: e.activation(
                            out=sg[:, 0, :], in_=f_sb[:, ts, n * 512:(n + 1) * 512], func=AF.Square,
                            accum_out=small[:, 8 + ts * 8 + n:9 + ts * 8 + n]),
                            reads=[("f", ts, n)], writes=[("sg", 0), ("ssq", ts, n)])
                if tt == 0:
                    self.dbg(f_sb[:, 0, 0:512], 512, "f")
                    self.dbg(small[:, 0:64], 64, "small2")
                last = self.postnorm_residual(h_src, h_dst, tt, gidx + 1, 0.5, f_sb, small, hc, gc)
                ctx.barrier()
        return last

    def postnorm_residual(self, h_src, h_dst, tt, gidx, coef, f_sb, small, hc, gc):
        ctx, cfg = self.ctx, self.cfg
        D = cfg.D
        NB = D // 512
        CH = min(1024, D)
        last = None
        for ts in range(4):
            r0 = tt * TT + ts * 128
            ctx.op("dve", lambda e, ts=ts: e.tensor_reduce(
                out=small[:, 4:5], in_=small[:, 8 + ts * 8:8 + ts * 8 + NB], op=ALU.add, axis=AX.X),
                reads=[("ssq", ts, n) for n in range(NB)], writes=["tot"])
            ctx.op("dve", lambda e: e.tensor_scalar(out=small[:, 5:6], in0=small[:, 4:5],
                                                    scalar1=1.0 / D, scalar2=RMS_EPS,
                                                    op0=ALU.mult, op1=ALU.add),
                   reads=["tot"], writes=["t1"])
            ctx.op("act", lambda e: e.activation(out=small[:, 6:7], in_=small[:, 5:6], func=AF.Sqrt),
                   reads=["t1"], writes=["t2"])
            ctx.op("dve", lambda e: e.reciprocal(out=small[:, 7:8], in_=small[:, 6:7]),
                   reads=["t2"], writes=["rstd"])
            for c in range(D // CH):
                cs = slice(c * CH, (c + 1) * CH)
                hs = c % 2
                ctx.dma("sp", hc[:, hs, 0:CH], h_src[r0:r0 + 128, cs], reads=["hdram"],
                        writes=[("hc", hs)])
                ctx.dma("sp", gc[:, 0:CH], self.normg[gidx:gidx + 1, cs].broadcast_to([128, CH]),
                        writes=["gc"])
                ctx.op("dve", lambda e, ts=ts, cs=cs: e.scalar_tensor_tensor(
                    out=f_sb[:, ts, cs], in0=f_sb[:, ts, cs], scalar=small[:, 7:8], in1=gc[:, 0:CH],
                    op0=ALU.mult, op1=ALU.mult),
                    reads=[("f", ts, n) for n in range(NB)] + ["rstd", "gc"], writes=[("f2", ts, c)])
                ctx.op("dve", lambda e, ts=ts, cs=cs, hs=hs: e.scalar_tensor_tensor(
                    out=f_sb[:, ts, cs], in0=f_sb[:, ts, cs], scalar=coef, in1=hc[:, hs, 0:CH],
                    op0=ALU.mult, op1=ALU.add),
                    reads=[("f2", ts, c), ("hc", hs)], writes=[("f3", ts, c)])
            last = ctx.dma("sp", h_dst[r0:r0 + 128, :], f_sb[:, ts, :],
                           reads=[("f3", ts, c) for c in range(D // CH)], writes=["hdram_w"])
        return last


    def linear_A(self, wname, K_tiles, col0, ncols, rhs_fn, rhs_res, evac, wbuf, next_slot, ntok=TT, bw=512):
        ctx = self.ctx
        blk = 0
        for c0 in range(col0, col0 + ncols, bw):
            cw = min(bw, col0 + ncols - c0)
            nj = cw // 128
            b0 = (blk % 2) * (bw // 128)
            blk += 1
            for ks in range(0, K_tiles, 8):
                kn = min(8, K_tiles - ks)
                sl = next_slot()
                self.wtile(wname, ks, kn, c0, cw, wbuf, sl)
                for j in range(nj):
                    ctx.mm([(lambda e, kk=kk, j=j, sl=sl, ks=ks, b0=b0: e.matmul(
                        self.ps[b0 + j][:, 0:ntok], lhsT=wbuf[:, sl, kk, j * 128:(j + 1) * 128],
                        rhs=rhs_fn(ks + kk), start=(ks + kk == 0), stop=(ks + kk == K_tiles - 1)))
                        for kk in range(kn)],
                        reads=[("wbuf", sl)] + [(rhs_res, ks + kk) for kk in range(kn)],
                        writes=[("ps", b0 + j)])
            for j in range(nj):
                evac((c0 - col0) // 128 + j, b0 + j)

    def linear_B(self, wname, K_tiles, col0, ncols, lhs_fn, lhs_res, evac, wbuf, next_slot, nts=4):
        ctx = self.ctx
        nb = 0
        for c0 in range(col0, col0 + ncols, 512):
            cw = min(512, col0 + ncols - c0)
            bb = (nb % 2) * 4
            for js in range(0, K_tiles, 8):
                jn = min(8, K_tiles - js)
                sl = next_slot()
                self.wtile(wname, js, jn, c0, cw, wbuf, sl)
                for ts in range(nts):
                    ctx.mm([(lambda e, jj=jj, ts=ts, sl=sl, js=js, bb=bb, cw=cw: e.matmul(
                        self.ps[bb + ts][:, 0:cw], lhsT=lhs_fn(js + jj, ts),
                        rhs=wbuf[:, sl, jj, 0:cw], start=(js + jj == 0),
                        stop=(js + jj == K_tiles - 1))) for jj in range(jn)],
                        reads=[("wbuf", sl)] + [(lhs_res, js + jj) for jj in range(jn)],
                        writes=[("ps", bb + ts)])
            for ts in range(nts):
                evac(nb, ts, bb + ts, cw)
            nb += 1

    def sincos(self, turns_fn, n, sin_out, cos_out, ki, fr, halfpi, tag):
        ctx = self.ctx
        for (dst, add) in ((sin_out, 0.0), (cos_out, 0.25)):
            ctx.op("dve", lambda e, add=add: turns_fn(e, ki[:, 0:n], add), reads=[tag + "_in"],
                   writes=["kibuf"])
            ctx.op("dve", lambda e, add=add: turns_fn(e, fr[:, 0:n], add), reads=[tag + "_in"],
                   writes=["frbuf"])
            ctx.op("dve", lambda e: e.tensor_tensor(out=fr[:, 0:n], in0=fr[:, 0:n], in1=ki[:, 0:n],
                                                    op=ALU.subtract),
                   reads=["kibuf", "frbuf"], writes=["frbuf"])
            ctx.op("act", lambda e, dst=dst: e.activation(out=dst, in_=fr[:, 0:n], func=AF.Sin,
                                                          scale=TWO_PI),
                   reads=["frbuf"], writes=[tag + "_out"])

    def s5(self, h_src, h_dst, gidx):
        ctx, cfg, nc = self.ctx, self.cfg, self.nc
        D, KT, G = cfg.D, cfg.KT, cfg.G
        GP = G // 2
        from contextlib import ExitStack
        NW = 3
        S = self.s5in
        with ExitStack() as st0:
            uz = st0.enter_context(nc.sbuf_tensor("uz", [128, KT, TOK], BF16))
            small = st0.enter_context(nc.sbuf_tensor("small", [128, 64], F32))
            gcol = st0.enter_context(nc.sbuf_tensor("gcol", [128, 32], F32))
            dcol = st0.enter_context(nc.sbuf_tensor("dcol", [128, 32], F32))
            bglu = st0.enter_context(nc.sbuf_tensor("bglu", [128, 32], F32))
            slot = [0]

            def next_slot():
                s_ = slot[0]
                slot[0] = (s_ + 1) % NW
                return s_
            ctx.op("dve", lambda e: e.memset(small[:, :], 0.0), writes=["ss", "t1", "t2", "rstd", "tot"])
            ctx.dma("sp", dcol[:, 0:KT], S["d"].rearrange("(k p) -> p k", p=128), writes=["dcol"], slow=True)
            ctx.dma("sp", bglu[:, 0:KT], S["bglu"].rearrange("(k p) -> p k", p=128), writes=["bglu"], slow=True)
            with ExitStack() as st:
                R64 = st.enter_context(nc.sbuf_tensor("R64", [128, 7 * D // 2], F32))
                wbuf = st.enter_context(nc.sbuf_tensor("wbuf", [128, NW, 8, 512], BF16))
                Rb = R64[:, :].bitcast(BF16)
                xnT = Rb[:, 0:KT * TT].rearrange("p (k t) -> p k t", t=TT)
                hb = R64[:, 2 * D:3 * D]
                xb = Rb[:, 6 * D:7 * D]
                R = {"xnT": xnT, "hb": hb, "xb": xb, "gcol": gcol, "small": small}
                for tt in range(TOK // TT):
                    self.prenorm(h_src, tt, gidx, R, None)

                    def evac(jt, bank, tt=tt):
                        ctx.op("act", lambda e: e.activation(
                            out=uz[:, jt, tt * TT:(tt + 1) * TT], in_=self.ps[bank][:, :], func=AF.Copy),
                            reads=[("ps", bank)], writes=[("u", jt)])
                    self.linear_A("s5in", KT, 0, D, lambda k: xnT[:, k, :], "xnT", evac, wbuf, next_slot)
                    ctx.barrier()
            yscr = self.yscr.ap()
            with ExitStack() as st:
                sb = lambda n, shp, dt=F32: st.enter_context(nc.sbuf_tensor("s5_" + n, shp, dt))
                iota = sb("iota", [128, TOK])
                iota1 = sb("iota1", [128, TOK])
                msk = sb("msk", [128, 16])
                prm = sb("prm", [128, 40, GP])
                Yb = sb("Yb", [128, 2, GP, 32], BF16)
                Lst = sb("Lst", [128, 2, 4, 128], BF16)
                Lb = sb("Lb", [128, 2, 4, 128], BF16)
                Lc = sb("Lc", [128, 2, 128], BF16)
                cnat = sb("cnat", [128, 2, 64])
                Z = sb("Z", [128, 128])
                ki = sb("ki", [128, TOK], I32); fr = sb("fr", [128, TOK])
                wend = sb("wend", [128, 2, GP])
                wall = sb("wall", [128, NCORES, 2, GP])
                sel = sb("sel", [128, NCORES * 3])
                hp = sb("hp", [128, 1])
                stS = st.enter_context(ExitStack())
                sbS = lambda n, shp, dt=F32: stS.enter_context(nc.sbuf_tensor("s5_" + n, shp, dt))
                bre = sbS("bre", [128, GP, 16]); bim = sbS("bim", [128, GP, 16])
                bbr = sbS("bbr", [128, GP, 16]); bbi = sbS("bbi", [128, GP, 16])
                tmpb = sbS("tmpb", [128, GP, 16])
                ctx.op("dve", lambda e: e.memset(hp[:, :], math.pi / 2), writes=["hp"])
                ctx.dma("sp", iota[:, :], self.cst["iota"], writes=["iota"])
                ctx.dma("sp", iota1[:, :], self.cst["iota1"], writes=["iota1"])
                ctx.dma("sp", msk[:, :], self.cst["msk"], writes=["msk"])
                ctx.dma("sp", sel[:, :], self.cst["sel"], writes=["sel"])
                ctx.dma("sp", prm[:, 0, :], S["a_re"].rearrange("(gp q) -> q gp", q=128), writes=["are"], slow=True)
                ctx.dma("sp", prm[:, 1, :], S["a_im"].rearrange("(gp q) -> q gp", q=128), writes=["aim"], slow=True)
                ld = S["log_dt"].rearrange("(gp g2) -> g2 gp", g2=2)
                for g2 in range(2):
                    ctx.dma("sp", prm[g2 * 64:(g2 + 1) * 64, 2, :], ld[g2:g2 + 1, :].broadcast_to([64, GP]),
                            writes=[("ldt", g2)], slow=True)
                ctx.dma("sp", bre[:, :, :], S["b_re"].rearrange("(gp q h) -> q gp h", q=128, h=16), writes=["bre"])
                ctx.dma("sp", bim[:, :, :], S["b_im"].rearrange("(gp q h) -> q gp h", q=128, h=16), writes=["bim"])
                ARE, AIM, LDT, DT, LNR, PH, MAG, SNP, CSP, ABR, ABI, DEN, FRE, FIM, TA, TB, TC, C1023, S1023, \
                    L1R, L1I, VRE, VIM, TD = range(24)
                pv = lambda i_: prm[:, i_, :]

                def dve(fn, reads, writes):
                    return ctx.op("dve", fn, reads=reads, writes=writes)

                def tt_(o, a, b, op):
                    dve(lambda e: e.tensor_tensor(out=pv(o), in0=pv(a), in1=pv(b), op=op),
                        [("p", a), ("p", b)], [("p", o)])

                def ts_(o, a, s1, s2=None, op0=ALU.mult, op1=None):
                    dve(lambda e: e.tensor_scalar(out=pv(o), in0=pv(a), scalar1=s1, scalar2=s2, op0=op0,
                                                  **({"op1": op1} if op1 is not None else {})),
                        [("p", a)], [("p", o)])
                for nm, idx in (("are", ARE), ("aim", AIM)):
                    ctx.last_w[("p", idx)] = ctx.last_w[nm]
                ctx.op("act", lambda e: e.activation(out=pv(DT), in_=pv(LDT), func=AF.Exp),
                       reads=[("ldt", 0), ("ldt", 1)], writes=[("p", DT)])
                tt_(LNR, DT, ARE, ALU.mult)
                tt_(PH, DT, AIM, ALU.mult)
                ts_(PH, PH, 1.0 / TWO_PI)
                ctx.op("act", lambda e: e.activation(out=pv(MAG), in_=pv(LNR), func=AF.Exp),
                       reads=[("p", LNR)], writes=[("p", MAG)])

                def small_sincos(turn_idx, mult, sn_idx, cs_idx, tag):
                    def tf(e, out, add):
                        return e.tensor_scalar(out=out, in0=pv(turn_idx), scalar1=float(mult), scalar2=float(add),
                                               op0=ALU.mult, op1=ALU.add)
                    ctx.last_w[tag + "_in"] = ctx.last_w[("p", turn_idx)]
                    self.sincos(tf, GP, pv(sn_idx), None, ki, fr, hp, tag)
                    ctx.last_w[("p", sn_idx)] = ctx.last_w[tag + "_out"]
                def small_sc(turn_idx, mult, sn_idx, cs_idx, tag):
                    def tf(e, out, add):
                        return e.tensor_scalar(out=out, in0=pv(turn_idx), scalar1=float(mult), scalar2=float(add),
                                               op0=ALU.mult, op1=ALU.add)
                    ctx.last_w[tag + "_in"] = ctx.last_w[("p", turn_idx)]
                    self.sincos(tf, GP, pv(sn_idx), pv(cs_idx), ki, fr, hp, tag)
                    ctx.last_w[("p", sn_idx)] = ctx.last_w[tag + "_out"]
                    ctx.last_w[("p", cs_idx)] = ctx.last_w[tag + "_out"]
                small_sc(PH, 1.0, SNP, CSP, "sc1")
                tt_(ABR, MAG, CSP, ALU.mult)
                tt_(ABI, MAG, SNP, ALU.mult)
                tt_(TA, ARE, ARE, ALU.mult)
                tt_(TB, AIM, AIM, ALU.mult)
                tt_(DEN, TA, TB, ALU.add)
                dve(lambda e: e.reciprocal(out=pv(DEN), in_=pv(DEN)), [("p", DEN)], [("p", DEN)])
                ts_(TC, ABR, -1.0, None, op0=ALU.add)
                tt_(TA, TC, ARE, ALU.mult)
                tt_(TB, ABI, AIM, ALU.mult)
                tt_(FRE, TA, TB, ALU.add)
                tt_(FRE, FRE, DEN, ALU.mult)
                tt_(TA, ABI, ARE, ALU.mult)
                tt_(TB, TC, AIM, ALU.mult)
                tt_(FIM, TA, TB, ALU.subtract)
                tt_(FIM, FIM, DEN, ALU.mult)
                fb = lambda i_: prm[:, i_, :].unsqueeze(2).broadcast_to([128, GP, 16])

                def bb_(o, a, fi, op_):
                    pass
                dve(lambda e: e.tensor_tensor(out=bbr[:, :, :], in0=bre[:, :, :], in1=fb(FRE), op=ALU.mult),
                    ["bre", ("p", FRE)], ["bbr"])
                dve(lambda e: e.tensor_tensor(out=tmpb[:, :, :], in0=bim[:, :, :], in1=fb(FIM), op=ALU.mult),
                    ["bim", ("p", FIM)], ["tmpb"])
                dve(lambda e: e.tensor_tensor(out=bbr[:, :, :], in0=bbr[:, :, :], in1=tmpb[:, :, :], op=ALU.subtract),
                    ["bbr", "tmpb"], ["bbr"])
                dve(lambda e: e.tensor_tensor(out=bbi[:, :, :], in0=bim[:, :, :], in1=fb(FRE), op=ALU.mult),
                    ["bim", ("p", FRE)], ["bbi"])
                dve(lambda e: e.tensor_tensor(out=tmpb[:, :, :], in0=bre[:, :, :], in1=fb(FIM), op=ALU.mult),
                    ["bre", ("p", FIM)], ["tmpb"])
                dve(lambda e: e.tensor_tensor(out=bbi[:, :, :], in0=bbi[:, :, :], in1=tmpb[:, :, :], op=ALU.add),
                    ["bbi", "tmpb"], ["bbi"])
                dve(lambda e: e.memset(Yb[:, :, :, :], 0.0), [], ["Yb"])
                for c_, src in ((0, bbr), (1, bbi)):
                    for g2 in range(2):
                        dve(lambda e, c_=c_, src=src, g2=g2: e.tensor_copy(
                            out=Yb[g2 * 64:(g2 + 1) * 64, c_, :, g2 * 16:(g2 + 1) * 16],
                            in_=src[g2 * 64:(g2 + 1) * 64, :, :]), ["bbr", "bbi", "Yb"], ["Yb"])
                ctx.barrier()
                stS.close()
                Ct = sb("Ct", [128, TOK]); Sn = sb("Sn", [128, TOK])
                t1 = sb("t1", [128, TOK]); t2 = sb("t2", [128, TOK])
                ere = sb("ere", [128, TOK]); eim = sb("eim", [128, TOK])
                wre = sb("wre", [128, TOK]); wim = sb("wim", [128, TOK])
                P = sb("P", [128, 4, TOK], BF16)
                ysb = sb("ysb", [128, TOK])
                for ct in range(KT):
                    for c_ in range(2):
                        bank = 6 + c_
                        pTb = self.ps[bank][:, :].bitcast(BF16)
                        ctx.mm([lambda e, c_=c_, pTb=pTb, ct=ct: e.transpose(
                            out=pTb[:, 0:128],
                            in_=Yb[:, c_, ct * 4:ct * 4 + 4, :].rearrange("q a b -> q (a b)"),
                            identity=self.ident[:, :])],
                            reads=["Yb", "ident"], writes=[("ps", bank)])
                        for pp in range(4):
                            dve(lambda e, c_=c_, pp=pp, pTb=pTb: e.tensor_scalar(
                                out=Lb[:, c_, pp, :], in0=pTb[:, 0:128],
                                scalar1=msk[:, 8 + pp:9 + pp], scalar2=None, op0=ALU.mult),
                                [("ps", bank), "msk"], [("Lb", c_, pp)])
                    self.s5_build_C(ct, S, cnat, Z, msk, Lst)
                    for pp in range(4):
                        gp = ct * 4 + pp
                        self.s5_pair_main(ct, pp, gp, uz, Lb, Lst, prm, PH, MAG, iota, Ct, Sn, ki, fr, hp,
                                          t1, t2, ere, eim, wre, wim, P, wend)
                    for hf in range(2):
                        dve(lambda e, hf=hf, ct=ct: e.scalar_tensor_tensor(
                            out=ysb[:, hf * 512:(hf + 1) * 512], in0=uz[:, ct, hf * 512:(hf + 1) * 512],
                            scalar=dcol[:, ct:ct + 1], in1=self.ps[4 + hf][:, :], op0=ALU.mult, op1=ALU.add),
                            [("ps", 4 + hf), ("u", ct), "dcol"], [("ysb", hf)])
                    ctx.dma("sp", yscr[ct * 128:(ct + 1) * 128, :], ysb[:, :], reads=[("ysb", 0), ("ysb", 1)],
                            writes=[("yscr", ct)])
                ctx.barrier()
                self.s5_carry(prm, wend, wall, sel, ki, fr, hp, iota,
                              dict(PH=PH, LNR=LNR, C1023=C1023, S1023=S1023, L1R=L1R, L1I=L1I, VRE=VRE,
                                   VIM=VIM, TA=TA, TB=TB, TC=TC, TD=TD, SNP=SNP, CSP=CSP, MAG=MAG), GP)
                for ct in range(KT):
                    self.s5_build_C(ct, S, cnat, Z, msk, Lst)
                    for pp in range(4):
                        gp = ct * 4 + pp
                        self.s5_pair_corr(ct, pp, gp, Lst, Lc, prm, PH, LNR, VRE, VIM, iota, iota1, Ct, Sn, ki, fr,
                                          hp, t1, t2, P)
                    ctx.dma("sp", ysb[:, :], yscr[ct * 128:(ct + 1) * 128, :], reads=[("yscr", ct)],
                            writes=[("ysb", 0), ("ysb", 1)])
                    T1 = [("t1", 0), ("t1", 1)]
                    T2 = [("t2", 0), ("t2", 1)]
                    YS = [("ysb", 0), ("ysb", 1)]
                    for hf in range(2):
                        hs = slice(hf * 512, (hf + 1) * 512)
                        dve(lambda e, hf=hf, hs=hs: e.tensor_tensor(out=ysb[:, hs], in0=ysb[:, hs],
                                                                    in1=self.ps[4 + hf][:, :], op=ALU.add),
                            [("ysb", hf), ("ps", 4 + hf)], [("ysb", hf)])
                    dve(lambda e: e.tensor_tensor(out=t1[:, :], in0=ysb[:, :], in1=ysb[:, :], op=ALU.mult), YS, T1)
                    dve(lambda e: e.tensor_scalar(out=t1[:, :], in0=t1[:, :], scalar1=0.044715, scalar2=1.0,
                                                  op0=ALU.mult, op1=ALU.add), T1, T1)
                    dve(lambda e: e.tensor_tensor(out=t1[:, :], in0=t1[:, :], in1=ysb[:, :], op=ALU.mult),
                        T1 + YS, T1)
                    ctx.op("act", lambda e: e.activation(out=t2[:, :], in_=t1[:, :], func=AF.Sigmoid,
                                                         scale=2.0 * math.sqrt(2.0 / math.pi)),
                           reads=T1, writes=T2)
                    dve(lambda e, ct=ct: e.tensor_tensor(out=uz[:, ct, :], in0=t2[:, :], in1=ysb[:, :], op=ALU.mult),
                        T2 + YS, [("z", ct)])
                ctx.barrier()
            with ExitStack() as st:
                zz = st.enter_context(nc.sbuf_tensor("zz", [128, KT, TT], BF16))
                wbuf = st.enter_context(nc.sbuf_tensor("wbuf", [128, NW, 8, 512], BF16))
                f_sbt = st.enter_context(nc.sbuf_tensor("f_sb", [128, 4 * D], F32))
                f_sb = f_sbt[:, :].rearrange("p (s d) -> p s d", d=D)
                sg = st.enter_context(nc.sbuf_tensor("sg", [128, 4, TT], F32))
                hc = st.enter_context(nc.sbuf_tensor("hc", [128, 2, 1024], F32))
                gc = st.enter_context(nc.sbuf_tensor("gc", [128, 1024], F32))
                for tt in range(TOK // TT):
                    tsl = slice(tt * TT, (tt + 1) * TT)

                    def evac(jt, bank, tsl=tsl):
                        ctx.op("act", lambda e: e.activation(out=sg[:, jt % 4, :], in_=self.ps[bank][:, :],
                                                             func=AF.Sigmoid, bias=bglu[:, jt:jt + 1]),
                               reads=[("ps", bank), "bglu"], writes=[("sg", jt % 4)])
                        ctx.op("dve", lambda e: e.tensor_tensor(out=zz[:, jt, :], in0=sg[:, jt % 4, :],
                                                                in1=uz[:, jt, tsl], op=ALU.mult),
                               reads=[("sg", jt % 4)], writes=[("zz", jt)])
                    self.linear_A("s5glu", KT, 0, D, lambda k, tsl=tsl: uz[:, k, tsl], "z", evac, wbuf, next_slot)

                    def evac2(n, ts, bank, cw):
                        ctx.op("dve", lambda e: e.tensor_copy(out=f_sb[:, ts, n * 512:n * 512 + cw],
                                                              in_=self.ps[bank][:, 0:cw]),
                               reads=[("ps", bank)], writes=[("f", ts, n)])
                        ctx.op("act", lambda e: e.activation(
                            out=sg[:, 0, 0:cw], in_=f_sb[:, ts, n * 512:n * 512 + cw], func=AF.Square,
                            accum_out=small[:, 8 + ts * 8 + n:9 + ts * 8 + n]),
                            reads=[("f", ts, n)], writes=[("sg", 0), ("ssq", ts, n)])
                    self.linear_B("s5out", KT, 0, D, lambda k, ts: zz[:, k, ts * 128:(ts + 1) * 128], "zz",
                                  evac2, wbuf, next_slot)
                    self.postnorm_residual(h_src, h_dst, tt, gidx + 1, 1.0, f_sb, small, hc, gc)
                    ctx.barrier()

    def s5_build_C(self, ct, S, cnat, Z, msk, Lst):
        ctx = self.ctx
        dve = lambda fn, r, w: ctx.op("dve", fn, reads=r, writes=w)
        ctx.dma("sp", cnat[:, 0, :], S["c_re"].rearrange("(r p) -> r p", p=64)[ct * 128:(ct + 1) * 128, :],
                writes=[("cnat", 0)])
        ctx.dma("sp", cnat[:, 1, :], S["c_im"].rearrange("(r p) -> r p", p=64)[ct * 128:(ct + 1) * 128, :],
                writes=[("cnat", 1)])
        for pp in range(4):
            for c_ in range(2):
                sgn = 1.0 if c_ == 0 else -1.0
                for g2 in range(2):
                    dve(lambda e, c_=c_, g2=g2, pp=pp, sgn=sgn: e.tensor_scalar(
                        out=Z[:, g2 * 64:(g2 + 1) * 64], in0=cnat[:, c_, :],
                        scalar1=msk[:, 2 * pp + g2:2 * pp + g2 + 1], scalar2=sgn,
                        op0=ALU.mult, op1=ALU.mult),
                        [("cnat", c_), "msk"], [("Z", g2)])
                bank = 6 + (c_ % 2)
                ctx.mm([lambda e, bank=bank: e.transpose(out=self.ps[bank][:, 0:128], in_=Z[:, :],
                                                          identity=self.identf[:, :])],
                       reads=[("Z", 0), ("Z", 1), "identf"], writes=[("ps", bank)])
                ctx.op("act", lambda e, c_=c_, pp=pp, bank=bank: e.activation(
                    out=Lst[:, c_, pp, :], in_=self.ps[bank][:, 0:128], func=AF.Copy),
                    reads=[("ps", bank)], writes=[("Lst", c_, pp)])

    def s5_tables(self, gp, prm, PH, iota, Ct, Sn, ki, fr, hp):
        ctx = self.ctx
        ph = prm[:, PH, gp:gp + 1]

        def tf(e, out, add):
            return e.tensor_scalar(out=out, in0=iota[:, :], scalar1=ph, scalar2=float(add),
                                   op0=ALU.mult, op1=ALU.add)
        ctx.last_w["tb_in"] = ctx.last_w.get(("p", PH))
        self.sincos(tf, TOK, Sn[:, :], Ct[:, :], ki, fr, hp, "tb")

    def s5_pair_main(self, ct, pp, gp, uz, Lb, Lst, prm, PH, MAG, iota, Ct, Sn, ki, fr, hp,
                     t1, t2, ere, eim, wre, wim, P, wend):
        ctx = self.ctx
        dve = lambda fn, r, w: ctx.op("dve", fn, reads=r, writes=w)
        for c_ in range(2):
            for hf in range(2):
                bank = 2 * c_ + hf
                ctx.mm([lambda e, c_=c_, hf=hf, bank=bank: e.matmul(
                    self.ps[bank][:, :], lhsT=Lb[:, c_, pp, :], rhs=uz[:, ct, hf * 512:(hf + 1) * 512],
                    start=True, stop=True)],
                    reads=[("Lb", c_, pp), ("u", ct)], writes=[("ps", bank)])
        self.s5_tables(gp, prm, PH, iota, Ct, Sn, ki, fr, hp)
        T = "tb_out"
        for hf in range(2):
            hs = slice(hf * 512, (hf + 1) * 512)
            bre_, bim_ = self.ps[hf][:, :], self.ps[2 + hf][:, :]
            dve(lambda e, hs=hs, bre_=bre_: e.tensor_tensor(out=t1[:, hs], in0=Ct[:, hs], in1=bre_, op=ALU.mult),
                [T, ("ps", hf)], [("t1", hf)])
            dve(lambda e, hs=hs, bim_=bim_: e.tensor_tensor(out=t2[:, hs], in0=Sn[:, hs], in1=bim_, op=ALU.mult),
                [T, ("ps", 2 + hf)], [("t2", hf)])
            dve(lambda e, hs=hs: e.tensor_tensor(out=ere[:, hs], in0=t1[:, hs], in1=t2[:, hs], op=ALU.add),
                [("t1", hf), ("t2", hf)], [("ere", hf)])
            dve(lambda e, hs=hs, bim_=bim_: e.tensor_tensor(out=t1[:, hs], in0=Ct[:, hs], in1=bim_, op=ALU.mult),
                [T, ("ps", 2 + hf)], [("t1", hf)])
            dve(lambda e, hs=hs, bre_=bre_: e.tensor_tensor(out=t2[:, hs], in0=Sn[:, hs], in1=bre_, op=ALU.mult),
                [T, ("ps", hf)], [("t2", hf)])
            dve(lambda e, hs=hs: e.tensor_tensor(out=eim[:, hs], in0=t1[:, hs], in1=t2[:, hs], op=ALU.subtract),
                [("t1", hf), ("t2", hf)], [("eim", hf)])
        rb = prm[:, MAG, gp:gp + 1].broadcast_to([128, TOK])
        dve(lambda e: e.tensor_tensor_scan(out=wre[:, :], data0=rb, data1=ere[:, :], initial=0.0,
                                           op0=ALU.mult, op1=ALU.add),
            [("ere", 0), ("ere", 1), ("p", MAG)], ["wre"])
        dve(lambda e: e.tensor_tensor_scan(out=wim[:, :], data0=rb, data1=eim[:, :], initial=0.0,
                                           op0=ALU.mult, op1=ALU.add),
            [("eim", 0), ("eim", 1), ("p", MAG)], ["wim"])
        dve(lambda e: e.tensor_copy(out=wend[:, 0, gp:gp + 1], in_=wre[:, TOK - 1:TOK]), ["wre"], [("wend", gp, 0)])
        dve(lambda e: e.tensor_copy(out=wend[:, 1, gp:gp + 1], in_=wim[:, TOK - 1:TOK]), ["wim"], [("wend", gp, 1)])
        for i_, (tab, w_, wn, sg_) in enumerate(((Ct, wre, "wre", 1.0), (Sn, wim, "wim", -1.0),
                                                 (Sn, wre, "wre", 1.0), (Ct, wim, "wim", 1.0))):
            dve(lambda e, i_=i_, tab=tab, w_=w_, sg_=sg_: e.scalar_tensor_tensor(
                out=P[:, i_, :], in0=tab[:, :], scalar=sg_, in1=w_[:, :], op0=ALU.mult, op1=ALU.mult),
                [T, wn], [("P", i_)])
        for hf in range(2):
            hs = slice(hf * 512, (hf + 1) * 512)
            fns = []
            for i_, li in enumerate((0, 0, 1, 1)):
                fns.append(lambda e, i_=i_, li=li, hs=hs, hf=hf: e.matmul(
                    self.ps[4 + hf][:, :], lhsT=Lst[:, li, pp, :], rhs=P[:, i_, hs],
                    start=(pp == 0 and i_ == 0), stop=(pp == 3 and i_ == 3)))
            ctx.mm(fns, reads=[("P", 0), ("P", 1), ("P", 2), ("P", 3), ("Lst", 0, pp), ("Lst", 1, pp)],
                   writes=[("ps", 4 + hf)])

    def s5_pair_corr(self, ct, pp, gp, Lst, Lc, prm, PH, LNR, VRE, VIM, iota, iota1, Ct, Sn, ki, fr, hp, t1, t2, P):
        ctx = self.ctx
        dve = lambda fn, r, w: ctx.op("dve", fn, reads=r, writes=w)
        self.s5_tables(gp, prm, PH, iota, Ct, Sn, ki, fr, hp)
        T = "tb_out"
        ctx.op("act", lambda e: e.activation(out=t1[:, :], in_=iota1[:, :], func=AF.Exp,
                                             scale=prm[:, LNR, gp:gp + 1]),
               reads=["iota1", ("p", LNR)], writes=[("t1", 0), ("t1", 1)])
        dve(lambda e: e.tensor_tensor(out=P[:, 0, :], in0=Ct[:, :], in1=t1[:, :], op=ALU.mult), [T, ("t1", 0), ("t1", 1)], [("P", 0)])
        dve(lambda e: e.tensor_tensor(out=P[:, 1, :], in0=Sn[:, :], in1=t1[:, :], op=ALU.mult), [T, ("t1", 0), ("t1", 1)], [("P", 1)])
        vre, vim = prm[:, VRE, gp:gp + 1], prm[:, VIM, gp:gp + 1]
        dve(lambda e: e.tensor_scalar(out=Lc[:, 0, :], in0=Lst[:, 0, pp, :], scalar1=vre, scalar2=None, op0=ALU.mult),
            [("p", VRE), ("Lst", 0, pp)], [("Lc", 0)])
        dve(lambda e: e.scalar_tensor_tensor(out=Lc[:, 0, :], in0=Lst[:, 1, pp, :], scalar=vim, in1=Lc[:, 0, :],
                                             op0=ALU.mult, op1=ALU.add),
            [("p", VIM), ("Lst", 1, pp), ("Lc", 0)], [("Lc", 0)])
        dve(lambda e: e.tensor_scalar(out=Lc[:, 1, :], in0=Lst[:, 0, pp, :], scalar1=vim, scalar2=-1.0,
                                      op0=ALU.mult, op1=ALU.mult),
            [("p", VIM), ("Lst", 0, pp)], [("Lc", 1)])
        dve(lambda e: e.scalar_tensor_tensor(out=Lc[:, 1, :], in0=Lst[:, 1, pp, :], scalar=vre, in1=Lc[:, 1, :],
                                             op0=ALU.mult, op1=ALU.add),
            [("p", VRE), ("Lst", 1, pp), ("Lc", 1)], [("Lc", 1)])
        for hf in range(2):
            hs = slice(hf * 512, (hf + 1) * 512)
            ctx.mm([lambda e, i_=i_, hs=hs, hf=hf: e.matmul(
                self.ps[4 + hf][:, :], lhsT=Lc[:, i_, :], rhs=P[:, i_, hs],
                start=(pp == 0 and i_ == 0), stop=(pp == 3 and i_ == 1)) for i_ in range(2)],
                reads=[("P", 0), ("P", 1), ("Lc", 0), ("Lc", 1)], writes=[("ps", 4 + hf)])

    def s5_carry(self, prm, wend, wall, sel, ki, fr, hp, iota, I, GP):
        ctx, nc = self.ctx, self.nc
        pv = lambda i_: prm[:, i_, :]
        dve = lambda fn, r, w: ctx.op("dve", fn, reads=r, writes=w)
        X = {n: 24 + i for i, n in enumerate(["A0R", "A0I", "A1R", "A1I", "A2R", "A2I", "L2R", "L2I", "SR", "SI",
                                              "U1", "U2", "M1", "C1K", "S1K"])}

        def tt_(o, a, b, op):
            dve(lambda e: e.tensor_tensor(out=pv(o), in0=pv(a), in1=pv(b), op=op), [("p", a), ("p", b)], [("p", o)])

        def sc(turn_idx, mult, sn_idx, cs_idx, tag):
            def tf(e, out, add):
                return e.tensor_scalar(out=out, in0=pv(turn_idx), scalar1=float(mult), scalar2=float(add),
                                       op0=ALU.mult, op1=ALU.add)
            ctx.last_w[tag + "_in"] = ctx.last_w.get(("p", turn_idx))
            self.sincos(tf, GP, pv(sn_idx), pv(cs_idx), ki, fr, hp, tag)
            ctx.last_w[("p", sn_idx)] = ctx.last_w[tag + "_out"]
            ctx.last_w[("p", cs_idx)] = ctx.last_w[tag + "_out"]
        sc(I["PH"], float(TOK - 1), I["S1023"], I["C1023"], "sc2")
        wr, wi = wend[:, 0, :], wend[:, 1, :]
        allw = [("wend", g, c) for g in range(GP) for c in range(2)]
        dve(lambda e: e.tensor_tensor(out=pv(X["U1"]), in0=pv(I["C1023"]), in1=wr, op=ALU.mult), allw + [("p", I["C1023"])], [("p", X["U1"])])
        dve(lambda e: e.tensor_tensor(out=pv(X["U2"]), in0=pv(I["S1023"]), in1=wi, op=ALU.mult), allw + [("p", I["S1023"])], [("p", X["U2"])])
        tt_(X["SR"], X["U1"], X["U2"], ALU.subtract)
        dve(lambda e: e.tensor_tensor(out=pv(X["U1"]), in0=pv(I["S1023"]), in1=wr, op=ALU.mult), allw + [("p", I["S1023"])], [("p", X["U1"])])
        dve(lambda e: e.tensor_tensor(out=pv(X["U2"]), in0=pv(I["C1023"]), in1=wi, op=ALU.mult), allw + [("p", I["C1023"])], [("p", X["U2"])])
        tt_(X["SI"], X["U1"], X["U2"], ALU.add)
        dve(lambda e: e.tensor_copy(out=wend[:, 0, :], in_=pv(X["SR"])), [("p", X["SR"])], ["wendT"])
        dve(lambda e: e.tensor_copy(out=wend[:, 1, :], in_=pv(X["SI"])), [("p", X["SI"])], ["wendT"])
        if self.use_cc:
            ctx.dma("sp", self.wend_b.ap(), wend[:, :, :].rearrange("q c g -> q (c g)"), reads=["wendT"], writes=["wend_b"])
            ctx.collective(self.wend_b, self.wend_all, reads=["wend_b"], writes=["wend_all"])
            ctx.dma("sp", wall[:, :, :, :].rearrange("q r c g -> q r (c g)"),
                    self.wend_all.ap().rearrange("(r q) x -> q r x", q=128), reads=["wend_all"], writes=["wall"])
        else:
            dve(lambda e: e.memset(wall[:, :, :, :], 0.0), [], ["wall"])
        for d in range(3):
            for c_ in range(2):
                o = X["A%d%s" % (d, "RI"[c_])]
                dve(lambda e, o=o: e.memset(pv(o), 0.0), [], [("p", o)])
                for r in range(NCORES):
                    dve(lambda e, o=o, r=r, c_=c_, d=d: e.scalar_tensor_tensor(
                        out=pv(o), in0=wall[:, r, c_, :], scalar=sel[:, 3 * r + d:3 * r + d + 1], in1=pv(o),
                        op0=ALU.mult, op1=ALU.add), ["wall", "sel", ("p", o)], [("p", o)])
        dve(lambda e: e.tensor_scalar(out=pv(X["U1"]), in0=pv(I["LNR"]), scalar1=float(TOK), scalar2=None, op0=ALU.mult),
            [("p", I["LNR"])], [("p", X["U1"])])
        ctx.op("act", lambda e: e.activation(out=pv(X["M1"]), in_=pv(X["U1"]), func=AF.Exp),
               reads=[("p", X["U1"])], writes=[("p", X["M1"])])
        sc(I["PH"], float(TOK), X["S1K"], X["C1K"], "sc3")
        tt_(I["L1R"], X["M1"], X["C1K"], ALU.mult)
        tt_(I["L1I"], X["M1"], X["S1K"], ALU.mult)
        tt_(X["U1"], I["L1R"], I["L1R"], ALU.mult)
        tt_(X["U2"], I["L1I"], I["L1I"], ALU.mult)
        tt_(X["L2R"], X["U1"], X["U2"], ALU.subtract)
        tt_(X["U1"], I["L1R"], I["L1I"], ALU.mult)
        tt_(X["L2I"], X["U1"], X["U1"], ALU.add)
        for (lr, li, ar, ai) in ((I["L1R"], I["L1I"], X["A1R"], X["A1I"]), (X["L2R"], X["L2I"], X["A2R"], X["A2I"])):
            tt_(X["U1"], lr, ar, ALU.mult)
            tt_(X["U2"], li, ai, ALU.mult)
            tt_(X["U1"], X["U1"], X["U2"], ALU.subtract)
            tt_(X["A0R"], X["A0R"], X["U1"], ALU.add)
            tt_(X["U1"], lr, ai, ALU.mult)
            tt_(X["U2"], li, ar, ALU.mult)
            tt_(X["U1"], X["U1"], X["U2"], ALU.add)
            tt_(X["A0I"], X["A0I"], X["U1"], ALU.add)
        tt_(X["U1"], I["CSP"], X["A0R"], ALU.mult)
        tt_(X["U2"], I["SNP"], X["A0I"], ALU.mult)
        tt_(I["VRE"], X["U1"], X["U2"], ALU.subtract)
        tt_(X["U1"], I["SNP"], X["A0R"], ALU.mult)
        tt_(X["U2"], I["CSP"], X["A0I"], ALU.mult)
        tt_(I["VIM"], X["U1"], X["U2"], ALU.add)


    def attn(self, h_src, h_dst, gidx):
        ctx, cfg, nc = self.ctx, self.cfg, self.nc
        D, KT, NH, NKV, QPK = cfg.D, cfg.KT, cfg.NH, cfg.NKV, cfg.QPK
        KVW = cfg.KVW
        from contextlib import ExitStack
        NW = 3
        A = self.attin
        dve = lambda fn, r, w: ctx.op("dve", fn, reads=r, writes=w)
        act = lambda fn, r, w: ctx.op("act", fn, reads=r, writes=w)
        NB = TOK // 128
        HW = NKV * 128 + KVW
        slot = [0]

        def next_slot():
            s_ = slot[0]
            slot[0] = (s_ + 1) % NW
            return s_
        with ExitStack() as st0:
            sb0 = lambda n, shp, dt=F32: st0.enter_context(nc.sbuf_tensor("as_" + n, shp, dt))
            qT = sb0("qT", [128, KT, TOK], BF16)
            small = sb0("small", [128, 64])
            gcol = sb0("gcol", [128, 32])
            dve(lambda e: e.memset(small[:, :], 0.0), [], ["ss", "t1", "t2", "rstd", "tot"])
            with ExitStack() as st1:
                sb1 = lambda n, shp, dt=F32: st1.enter_context(nc.sbuf_tensor("as_" + n, shp, dt))
                kTd = sb1("kTd", [128, NKV, 128 + TOK], BF16)
                vtok = sb1("vtok", [128, NB + 1, KVW], BF16)
                sinkb = sb1("sinkb", [128, NH])
                hsel = sb1("hsel", [128, NCORES])
                maskt = sb1("mask", [128, 2, 256])
                ctx.dma("sp", sinkb[:, :], A["sinks"].rearrange("(o n) -> o n", o=1).broadcast_to([128, NH]),
                        writes=["sinkb"])
                ctx.dma("sp", hsel[:, :], self.cst["hsel"], writes=["hsel"])
                ctx.dma("sp", maskt[:, :, :], self.cst["amask"].rearrange("p (a b) -> p a b", a=2), writes=["mask"])
                with ExitStack() as st:
                    sb = lambda n, shp, dt=F32: st.enter_context(nc.sbuf_tensor("as_" + n, shp, dt))
                    R64 = sb("R64", [128, 7 * D // 2])
                    wbuf = sb("wbuf", [128, NW, 8, 512], BF16)
                    kT = sb("kT", [128, NKV // 2, TOK], BF16)
                    Rb = R64[:, :].bitcast(BF16)
                    xnT = Rb[:, 0:KT * TT].rearrange("p (k t) -> p k t", t=TT)
                    hb = R64[:, 2 * D:3 * D]
                    xb = Rb[:, 6 * D:7 * D]
                    R = {"xnT": xnT, "hb": hb, "xb": xb, "gcol": gcol, "small": small}
                    cosT = sb("cosT", [128, TT]); sinT = sb("sinT", [128, TT])
                    rotf = sb("rotf", [128, 128]); rotb = sb("rotb", [128, 128], BF16)
                    bq = sb("bq", [128, KT + NKV // 2])
                    bv = sb("bv", [128, KVW])
                    qf = sb("qf", [128, 2, TT], BF16)
                    r1 = sb("r1", [128, TT]); r2 = sb("r2", [128, TT])
                    ctx.dma("sp", rotf[:, :], self.cst["rotT"], writes=["rotf"])
                    dve(lambda e: e.tensor_copy(out=rotb[:, :], in_=rotf[:, :]), ["rotf"], ["rotb"])
                    ctx.dma("sp", bq[:, 0:KT + NKV // 2],
                            A["bqkv"][0:D + KVW].rearrange("(k p) -> p k", p=128), writes=["bq"], slow=True)
                    dve(lambda e: e.tensor_scalar(out=bq[:, 0:KT], in0=bq[:, 0:KT], scalar1=0.125, scalar2=None,
                                                  op0=ALU.mult), ["bq"], ["bq"])
                    ctx.dma("sp", bv[:, :], A["bqkv"][D + KVW:D + 2 * KVW].rearrange("(o n) -> o n", o=1)
                            .broadcast_to([128, KVW]), writes=["bv"])
                    for tt in range(TOK // TT):
                        tsl = slice(tt * TT, (tt + 1) * TT)
                        ctx.dma("sp", cosT[:, :], self.cst["cosT"][:, tsl], writes=["cosT"])
                        ctx.dma("sp", sinT[:, :], self.cst["sinT"][:, tsl], writes=["sinT"])
                        self.prenorm(h_src, tt, gidx, R, None)

                        def evac_qk(jt, bank, tt=tt, tsl=tsl, isq=True):
                            bcol = jt if isq else KT + jt
                            sc_ = 0.125 if isq else 1.0
                            i2 = jt % 2
                            act(lambda e: e.activation(out=qf[:, i2, :], in_=self.ps[bank][:, :], func=AF.Identity,
                                                       bias=bq[:, bcol:bcol + 1], scale=sc_),
                                [("ps", bank), "bq"], [("qf", i2)])
                            rb_ = 4 + i2
                            ctx.mm([lambda e: e.matmul(self.ps[rb_][:, :], lhsT=rotb[:, :], rhs=qf[:, i2, :],
                                                       start=True, stop=True)],
                                   reads=["rotb", ("qf", i2)], writes=[("ps", rb_)])
                            dve(lambda e: e.tensor_tensor(out=r1[:, :], in0=qf[:, i2, :], in1=cosT[:, :], op=ALU.mult),
                                [("qf", i2), "cosT"], ["r1"])
                            dve(lambda e: e.tensor_tensor(out=r2[:, :], in0=self.ps[rb_][:, :], in1=sinT[:, :],
                                                          op=ALU.mult), [("ps", rb_), "sinT"], ["r2"])
                            dst = qT[:, jt, tsl] if isq else kT[:, jt, tsl]
                            dve(lambda e: e.tensor_tensor(out=dst, in0=r1[:, :], in1=r2[:, :], op=ALU.add),
                                ["r1", "r2"], [("q" if isq else "k", jt)])
                        self.linear_A("wqkv", KT, 0, D, lambda k: xnT[:, k, :], "xnT", evac_qk, wbuf, next_slot, bw=256)
                        self.linear_A("wqkv", KT, D, KVW, lambda k: xnT[:, k, :], "xnT",
                                      lambda jt, bank: evac_qk(jt, bank, isq=False), wbuf, next_slot, bw=256)

                        def evac_v(n, ts, bank, cw, tt=tt):
                            blk = 1 + tt * 4 + ts
                            dve(lambda e: e.tensor_tensor(out=vtok[:, blk, :], in0=self.ps[bank][:, 0:KVW],
                                                          in1=bv[:, :], op=ALU.add),
                                [("ps", bank), "bv"], [("vtok", blk)])
                        self.linear_B("wqkv", KT, D + KVW, KVW, lambda k, ts: xnT[:, k, ts * 128:(ts + 1) * 128], "xnT",
                                      evac_v, wbuf, next_slot)
                        ctx.barrier()
                    for hk in range(NKV):
                        src = kT[(hk % 2) * 64:(hk % 2 + 1) * 64, hk // 2, :]
                        for half in range(2):
                            ctx.dma("sp", kTd[half * 64:(half + 1) * 64, hk, 128:128 + TOK], src,
                                    reads=[("k", hk // 2)], writes=[("kTd", hk, half)])
                    ctx.barrier()
                with ExitStack() as st:
                    sb = lambda n, shp, dt=F32: st.enter_context(nc.sbuf_tensor("as_" + n, shp, dt))
                    hst = sb("hst", [128, HW])
                    hrx = sb("hrx", [128, 2, HW])
                    hacc = sb("hacc", [128, HW])
                    dve(lambda e: e.tensor_copy(out=hst[:, 0:NKV * 128].rearrange("p (h t) -> p h t", t=128),
                                                in_=kTd[:, :, TOK:TOK + 128]), [], ["hst"])
                    dve(lambda e: e.tensor_copy(out=hst[:, NKV * 128:HW], in_=vtok[:, NB, :]), ["hst"], ["hst"])
                    dve(lambda e: e.memset(hacc[:, :], 0.0), [], ["hacc"])
                    if self.use_cc:
                        ctx.dma("sp", self.halo_b.ap(), hst[:, :], reads=["hst"], writes=["halo_b"])
                        ctx.collective(self.halo_b, self.halo_all, reads=["halo_b"], writes=["halo_all"])
                        hall = self.halo_all.ap().rearrange("(r q) x -> r q x", q=128)
                        for r in range(NCORES):
                            ctx.dma("sp", hrx[:, r % 2, :], hall[r], reads=["halo_all"], writes=[("hrx", r % 2)])
                            dve(lambda e, r=r: e.scalar_tensor_tensor(
                                out=hacc[:, :], in0=hrx[:, r % 2, :], scalar=hsel[:, r:r + 1], in1=hacc[:, :],
                                op0=ALU.mult, op1=ALU.add), [("hrx", r % 2), "hsel", "hacc"], ["hacc"])
                    dve(lambda e: e.tensor_copy(out=kTd[:, :, 0:128],
                                                in_=hacc[:, 0:NKV * 128].rearrange("p (h t) -> p h t", t=128)),
                        ["hacc"], ["kTdh"])
                    dve(lambda e: e.tensor_copy(out=vtok[:, 0, :], in_=hacc[:, NKV * 128:HW]), ["hacc"], [("vtok", 0)])
                    ctx.barrier()
                with ExitStack() as st:
                    sb = lambda n, shp, dt=F32: st.enter_context(nc.sbuf_tensor("as_" + n, shp, dt))
                    Sm = sb("Sm", [128, 2, 256])
                    Pf = sb("Pf", [128, 2, 256])
                    Pn = sb("Pn", [128, 2, 256], BF16)
                    PT = sb("PT", [128, 2, 2, 128], BF16)
                    sm = sb("sm", [128, 2, 8])
                    vpc = sb("vpc", [128, 2, NB + 1, 2, 128], BF16)
                    dve(lambda e: e.memset(vpc[:, :, :, :, :].rearrange("p a b c d -> p (a b c d)"), 0.0), [],
                        [("vpc", 0), ("vpc", 1)])
                    cache = {}
                    nxt = [0]
                    for jt in range(KT):
                        need = sorted({(2 * jt) // QPK, (2 * jt + 1) // QPK})
                        for hk in need:
                            if hk in cache:
                                continue
                            sl_ = nxt[0]
                            nxt[0] = (sl_ + 1) % 2
                            for k_ in [k_ for k_, v_ in cache.items() if v_ == sl_]:
                                del cache[k_]
                            cache[hk] = sl_
                            for e_ in range(2):
                                dve(lambda e, e_=e_, sl_=sl_, hk=hk: e.tensor_copy(
                                    out=vpc[:, sl_, :, e_, e_ * 64:(e_ + 1) * 64],
                                    in_=vtok[:, :, hk * 64:(hk + 1) * 64]), [("vpc", sl_)], [("vpc", sl_)])
                        for b in range(NB):
                            ob = 6 + (b % 2)
                            for e_ in range(2):
                                h_ = 2 * jt + e_
                                hk = h_ // QPK
                                ps_s = self.ps[e_]
                                pr = slice(e_ * 64, (e_ + 1) * 64)
                                ctx.mm([lambda e, e_=e_, jt=jt, b=b, hk=hk, pr=pr, ps_s=ps_s: e.matmul(
                                    ps_s[:, 0:256], lhsT=qT[pr, jt, b * 128:(b + 1) * 128],
                                    rhs=kTd[pr, hk, b * 128:b * 128 + 256], start=True, stop=True)],
                                    reads=[("q", jt), ("q2", jt, b)], writes=[("ps", e_)])
                                mi = 0 if b > 0 else 1
                                dve(lambda e, e_=e_, ps_s=ps_s, mi=mi: e.tensor_tensor(
                                    out=Sm[:, e_, :], in0=ps_s[:, 0:256], in1=maskt[:, mi, :], op=ALU.add),
                                    [("ps", e_), "mask"], [("Sm", e_)])
                                dve(lambda e, e_=e_: e.tensor_reduce(out=sm[:, e_, 0:1], in_=Sm[:, e_, :], op=ALU.max,
                                                                     axis=AX.X), [("Sm", e_)], [("sm0", e_)])
                                dve(lambda e, e_=e_, h_=h_: e.tensor_scalar(
                                    out=sm[:, e_, 1:2], in0=sm[:, e_, 0:1], scalar1=sinkb[:, h_:h_ + 1], scalar2=-1.0,
                                    op0=ALU.max, op1=ALU.mult), [("sm0", e_), "sinkb"], [("sm1", e_)])
                                act(lambda e, e_=e_: e.activation(out=Pf[:, e_, :], in_=Sm[:, e_, :], func=AF.Exp,
                                                                  bias=sm[:, e_, 1:2], accum_out=sm[:, e_, 2:3]),
                                    [("Sm", e_), ("sm1", e_)], [("Pf", e_), ("sm2", e_)])
                                act(lambda e, e_=e_, h_=h_: e.activation(out=sm[:, e_, 3:4], in_=sinkb[:, h_:h_ + 1],
                                                                         func=AF.Exp, bias=sm[:, e_, 1:2]),
                                    [("sm1", e_), "sinkb"], [("sm3", e_)])
                                dve(lambda e, e_=e_: e.tensor_tensor(out=sm[:, e_, 4:5], in0=sm[:, e_, 2:3],
                                                                     in1=sm[:, e_, 3:4], op=ALU.add),
                                    [("sm2", e_), ("sm3", e_)], [("sm4", e_)])
                                dve(lambda e, e_=e_: e.reciprocal(out=sm[:, e_, 5:6], in_=sm[:, e_, 4:5]),
                                    [("sm4", e_)], [("sm5", e_)])
                                dve(lambda e, e_=e_: e.tensor_scalar(out=Pn[:, e_, :], in0=Pf[:, e_, :],
                                                                     scalar1=sm[:, e_, 5:6], scalar2=None, op0=ALU.mult),
                                    [("Pf", e_), ("sm5", e_)], [("Pn", e_)])
                                tb = 2 + e_
                                pT = self.ps[tb][:, :].bitcast(BF16)
                                ctx.mm([(lambda e, kb=kb, e_=e_, pT=pT: e.transpose(
                                    out=pT[:, kb * 128:(kb + 1) * 128], in_=Pn[:, e_, kb * 128:(kb + 1) * 128],
                                    identity=self.ident[:, :])) for kb in range(2)],
                                    reads=[("Pn", e_), "ident"], writes=[("ps", tb)])
                                act(lambda e, e_=e_, pT=pT: e.activation(
                                    out=PT[:, e_, :, :].rearrange("p a b -> p (a b)"), in_=pT[:, 0:256], func=AF.Copy),
                                    [("ps", tb)], [("PT", e_)])
                            fns = []
                            rd = [("PT", 0), ("PT", 1)]
                            for e_ in range(2):
                                hk = (2 * jt + e_) // QPK
                                sl_ = cache[hk]
                                rd.append(("vpc", sl_))
                                for kb in range(2):
                                    fns.append(lambda e, e_=e_, kb=kb, sl_=sl_, b=b, ob=ob: e.matmul(
                                        self.ps[ob][:, 0:128], lhsT=vpc[:, sl_, b + kb, e_, :], rhs=PT[:, e_, kb, :],
                                        start=(e_ == 0 and kb == 0), stop=(e_ == 1 and kb == 1)))
                            ctx.mm(fns, reads=rd, writes=[("ps", ob)])
                            act(lambda e, jt=jt, b=b, ob=ob: e.activation(
                                out=qT[:, jt, b * 128:(b + 1) * 128], in_=self.ps[ob][:, 0:128], func=AF.Copy),
                                [("ps", ob)], [("q2", jt, b)])
                    ctx.barrier()
            with ExitStack() as st:
                sb = lambda n, shp, dt=F32: st.enter_context(nc.sbuf_tensor("as_" + n, shp, dt))
                wbuf = sb("wbuf", [128, NW, 8, 512], BF16)
                f_sbt = sb("f_sb", [128, 4 * D])
                f_sb = f_sbt[:, :].rearrange("p (s d) -> p s d", d=D)
                sg = sb("sg", [128, TT])
                hc = sb("hc", [128, 2, 1024]); gc = sb("gc", [128, 1024])
                bo = sb("bo", [128, D])
                ctx.dma("sp", bo[:, :], A["bo"].rearrange("(o n) -> o n", o=1).broadcast_to([128, D]), writes=["bo"])
                for tt in range(TOK // TT):
                    def evac2(n, ts, bank, cw):
                        dve(lambda e: e.tensor_tensor(out=f_sb[:, ts, n * 512:n * 512 + cw], in0=self.ps[bank][:, 0:cw],
                                                      in1=bo[:, n * 512:n * 512 + cw], op=ALU.add),
                            [("ps", bank), "bo"], [("f", ts, n)])
                        act(lambda e: e.activation(out=sg[:, 0:cw], in_=f_sb[:, ts, n * 512:n * 512 + cw], func=AF.Square,
                                                   accum_out=small[:, 8 + ts * 8 + n:9 + ts * 8 + n]),
                            [("f", ts, n)], [("sg", 0), ("ssq", ts, n)])
                    self.linear_B("wo", KT, 0, D,
                                  lambda k, ts, tt=tt: qT[:, k, tt * TT + ts * 128:tt * TT + (ts + 1) * 128], "oT",
                                  evac2, wbuf, next_slot)
                    self.postnorm_residual(h_src, h_dst, tt, gidx + 1, 1.0, f_sb, small, hc, gc)
                    ctx.barrier()


def host_consts(core):
    c = {"ident": np.eye(128, dtype=np.float32)}
    c["iota"] = np.broadcast_to(np.arange(TOK, dtype=np.float32), (128, TOK)).copy()
    c["iota1"] = c["iota"] + 1.0
    msk = np.zeros((128, 16), np.float32)
    rows = np.arange(128)
    for g8 in range(8):
        msk[:, g8] = (rows // 16 == g8)
    for pp in range(4):
        msk[:, 8 + pp] = (rows // 32 == pp)
    c["msk"] = msk
    sel = np.zeros((NCORES, 3), np.float32)
    qpos, b = core % 4, core // 4
    for r in range(NCORES):
        if r // 4 == b and r % 4 < qpos:
            sel[r, qpos - 1 - (r % 4)] = 1.0
    c["sel"] = np.broadcast_to(sel.reshape(1, -1), (128, NCORES * 3)).copy()
    hsel = np.zeros((NCORES,), np.float32)
    if qpos > 0:
        hsel[core - 1] = 1.0
    c["hsel"] = np.broadcast_to(hsel.reshape(1, -1), (128, NCORES)).copy()
    qq = np.arange(128)[:, None]
    kk = np.arange(256)[None, :]
    valid = (kk >= qq + 1) & (kk <= qq + 128)
    m_gen = np.where(valid, 0.0, -1e30).astype(np.float32)
    m_first = np.where(valid & ((kk >= 128) | (qpos > 0)), 0.0, -1e30).astype(np.float32)
    c["amask"] = np.concatenate([m_gen, m_first], axis=1)
    pos = (qpos * TOK + np.arange(TOK)).astype(np.float32)
    inv = (np.float32(500000.0) ** (-(np.arange(0, 16, 2, dtype=np.float32) / np.float32(16)))).astype(np.float32)
    ang = pos[None, :] * inv[:, None]
    cosT = np.ones((128, TOK), np.float32)
    sinT = np.zeros((128, TOK), np.float32)
    rot = np.zeros((128, 128), np.float32)
    for e_ in range(2):
        for d in range(16):
            cosT[e_ * 64 + d] = np.cos(ang[d % 8])
            sinT[e_ * 64 + d] = np.sin(ang[d % 8])
        for d in range(8):
            rot[e_ * 64 + d + 8, e_ * 64 + d] = -1.0
            rot[e_ * 64 + d, e_ * 64 + d + 8] = 1.0
    c["cosT"], c["sinT"], c["rotT"] = cosT, sinT, rot
    return c


def big_for_stages(stages):
    big = []
    for st in stages:
        if st[0] == "ffn":
            big += ["wg%d%d" % (st[1], st[2]), "wu%d%d" % (st[1], st[2]), "wd%d%d" % (st[1], st[2])]
        elif st[0] == "s5":
            big += ["s5in", "s5glu", "s5out"]
        elif st[0] == "attn":
            big += ["wqkv", "wo"]
    return big


def big_source(inp, name):
    if name[0] == "w" and name[1] in "gud" and len(name) == 4:
        key = {"g": "ffn_w_gate", "u": "ffn_w_up", "d": "ffn_w_down"}[name[1]]
        return inp[key][int(name[2]), int(name[3])]
    return {"s5in": lambda: inp["s5_w_in"][0], "s5glu": lambda: inp["s5_w_glu"][0],
            "s5out": lambda: inp["s5_w_out"][0], "wqkv": lambda: inp["attn_w_qkv"][0],
            "wo": lambda: inp["attn_w_o"][0]}[name]()


def make_in_maps(cfg, inp, big):
    D = cfg.D
    x = np.asarray(inp["x"], dtype=np.float32).reshape(NCORES, TOK, D)
    ng = np.ascontiguousarray(np.asarray(inp["norm_g"], dtype=np.float32).reshape(12, D))
    maps = []
    srcs = {n: np.asarray(big_source(inp, n), dtype=np.float32) for n in big}
    for c in range(NCORES):
        m = {"x": x[c], "norm_g": ng}
        m.update(host_consts(c))
        for n, w in srcs.items():
            r = w.shape[0] // NCORES
            m[n] = np.ascontiguousarray(w[c * r:(c + 1) * r])
        for n, k in (("log_dt", "s5_log_dt"), ("a_re", "s5_a_re"), ("a_im", "s5_a_im"), ("b_re", "s5_b_re"),
                     ("b_im", "s5_b_im"), ("c_re", "s5_c_re"), ("c_im", "s5_c_im"), ("d", "s5_d"),
                     ("bglu", "s5_b_glu")):
            m["s5_" + n] = np.ascontiguousarray(np.asarray(inp[k], dtype=np.float32)[0].reshape(-1))
        m["at_bqkv"] = np.ascontiguousarray(np.asarray(inp["attn_b_qkv"], dtype=np.float32)[0])
        m["at_sinks"] = np.ascontiguousarray(np.asarray(inp["attn_sinks"], dtype=np.float32)[0])
        m["at_bo"] = np.ascontiguousarray(np.asarray(inp["attn_b_o"], dtype=np.float32)[0])
        maps.append(m)
    return maps


ALL_STAGES = [("ffn", 0, 0), ("s5",), ("ffn", 0, 1), ("ffn", 1, 0), ("attn",), ("ffn", 1, 1)]


def kernel(**inputs):
    cfg = Cfg(D=4096, FF=11008)
    stages = ALL_STAGES
    big = big_for_stages(stages)
    prog = Prog(cfg, stages, big)
    nc = prog.build()
    in_maps = make_in_maps(cfg, inputs, big)
    try:
        res = run_bass_kernel_spmd(nc, in_maps, core_ids=list(range(NCORES)))
    except Exception:
        import time
        time.sleep(75.0)
        res = run_bass_kernel_spmd(nc, in_maps, core_ids=list(range(NCORES)))
    out = np.stack([np.asarray(res.results[c]["out"], dtype=np.float32) for c in range(NCORES)])
    return out.reshape(2, 4096, cfg.D)
```
